# Optimizing a Trainium2 kernel written in Bass

```python
import math
import jax
import jax.numpy as jnp
from jax import lax
import numpy as np

D_MODEL = 2048
BATCH = 4
SEQ = 2048
DEPTH = 4

HEAD_DIM = 128
N_GROUPS = 4
HEADS_PER_GROUP = D_MODEL // (N_GROUPS * HEAD_DIM)
GROUP_WIDTH = HEADS_PER_GROUP * HEAD_DIM
DIFF_QK_DIM = HEAD_DIM // 2
Q_BLOCK = 128
MOBA_BLOCK = 256
MOBA_TOPK = 3
MOBA_Q_CHUNK = 32
ROPE_THETA = 10000.0
NORM_EPS = 1e-6
DIFF_SUBLN_EPS = 1e-5
FFN_HIDDEN = -(-8 * D_MODEL // (3 * 256)) * 256
SPLIT_SIZES = (GROUP_WIDTH, GROUP_WIDTH, GROUP_WIDTH, HEADS_PER_GROUP,
               GROUP_WIDTH, GROUP_WIDTH, GROUP_WIDTH,
               GROUP_WIDTH, GROUP_WIDTH, GROUP_WIDTH,
               GROUP_WIDTH, GROUP_WIDTH, GROUP_WIDTH)
IN_COLS = sum(SPLIT_SIZES)

kernel_name = 'hybrid_parallel_heads_fox_moba_diff_stickbreak'


def _rmsnorm(x, g, eps=NORM_EPS):
    xf = x.astype(jnp.float32)
    y = xf * lax.rsqrt(jnp.mean(xf * xf, axis=-1, keepdims=True) + eps)
    return (y * g.astype(jnp.float32)).astype(x.dtype)


def _rope_tables(s_len, dim):
    inv = 1.0 / (ROPE_THETA ** (jnp.arange(0, dim, 2, dtype=jnp.float32) / dim))
    ang = jnp.arange(s_len, dtype=jnp.float32)[:, None] * inv[None, :]
    return jnp.cos(ang), jnp.sin(ang)


def _apply_rope(x, cos, sin):
    x1, x2 = jnp.split(x.astype(jnp.float32), 2, axis=-1)
    out = jnp.concatenate([x1 * cos - x2 * sin, x2 * cos + x1 * sin], axis=-1)
    return out.astype(x.dtype)


def _heads(a):
    b, s, w = a.shape
    return a.reshape(b, s, w // HEAD_DIM, HEAD_DIM).transpose(0, 2, 1, 3)


def _merge(a):
    b, h, s, d = a.shape
    return a.transpose(0, 2, 1, 3).reshape(b, s, h * d)


def _to_blocks(a, blk):
    b, h, s, d = a.shape
    return a.reshape(b, h, s // blk, blk, d).transpose(2, 0, 1, 3, 4)


def _from_blocks(o):
    n, b, h, blk, d = o.shape
    return o.transpose(1, 2, 0, 3, 4).reshape(b, h, n * blk, d)


def _split_cols(z):
    points = [int(p) for p in np.cumsum(SPLIT_SIZES)[:-1]]
    return jnp.split(z, points, axis=-1)


def _forgetting_attention(q, k, v, f_logit):
    s_len, d = q.shape[2], q.shape[3]
    nq = s_len // Q_BLOCK
    c = jnp.cumsum(jax.nn.log_sigmoid(f_logit.astype(jnp.float32)), axis=-1)
    kpos = jnp.arange(s_len)
    scale = d ** -0.5

    def one_block(args):
        qb, cb, q0 = args
        qpos = q0 + jnp.arange(Q_BLOCK)
        logits = jnp.einsum('bhqd,bhkd->bhqk', qb, k).astype(jnp.float32) * scale
        logits = logits + cb[..., :, None] - c[..., None, :]
        logits = jnp.where(kpos[None, :] <= qpos[:, None], logits, -jnp.inf)
        p = jax.nn.softmax(logits, axis=-1)
        return jnp.einsum('bhqk,bhkd->bhqd', p.astype(v.dtype), v)

    c_blocks = c.reshape(c.shape[0], c.shape[1], nq, Q_BLOCK).transpose(2, 0, 1, 3)
    out = lax.map(one_block, (_to_blocks(q, Q_BLOCK), c_blocks, jnp.arange(nq) * Q_BLOCK))
    return _from_blocks(out)


def _moba_attention(q, k, v):
    b, h, s_len, d = q.shape
    n_kb = -(-s_len // MOBA_BLOCK)
    pad = n_kb * MOBA_BLOCK - s_len
    kb = jnp.pad(k, ((0, 0), (0, 0), (0, pad), (0, 0))).reshape(b, h, n_kb, MOBA_BLOCK, d)
    vb = jnp.pad(v, ((0, 0), (0, 0), (0, pad), (0, 0))).reshape(b, h, n_kb, MOBA_BLOCK, d)
    k_mean = jnp.mean(kb.astype(jnp.float32), axis=3)
    top_k = min(MOBA_TOPK, n_kb)
    n_sel = top_k + 1
    blk_ids = jnp.arange(n_kb)
    offs = jnp.arange(MOBA_BLOCK)
    bi = jnp.arange(b)[:, None, None, None]
    hi = jnp.arange(h)[None, :, None, None]
    scale = d ** -0.5

    def one_chunk(args):
        qc, q0 = args
        qpos = q0 + jnp.arange(MOBA_Q_CHUNK)
        own = qpos // MOBA_BLOCK
        own_b = jnp.broadcast_to(own[None, None, :, None], (b, h, MOBA_Q_CHUNK, 1))
        gate = jnp.einsum('bhqd,bhnd->bhqn', qc.astype(jnp.float32), k_mean)
        gate = jnp.where(blk_ids[None, :] < own[:, None], gate, -jnp.inf)
        _, top = lax.top_k(gate, top_k)
        sel = jnp.concatenate([top, own_b], axis=-1)
        slot_ok = jnp.concatenate([top < own_b, jnp.ones_like(own_b, dtype=bool)], axis=-1)
        k_sel = kb[bi, hi, sel]
        v_sel = vb[bi, hi, sel]
        logits = jnp.einsum('bhqd,bhqnkd->bhqnk', qc, k_sel).astype(jnp.float32) * scale
        kpos = sel[..., None] * MOBA_BLOCK + offs
        mask = slot_ok[..., None] & (kpos <= qpos[None, None, :, None, None])
        logits = jnp.where(mask, logits, -jnp.inf).reshape(b, h, MOBA_Q_CHUNK, n_sel * MOBA_BLOCK)
        p = jax.nn.softmax(logits, axis=-1).reshape(b, h, MOBA_Q_CHUNK, n_sel, MOBA_BLOCK)
        return jnp.einsum('bhqnk,bhqnkd->bhqd', p.astype(v_sel.dtype), v_sel)

    nq = s_len // MOBA_Q_CHUNK
    out = lax.map(one_chunk, (_to_blocks(q, MOBA_Q_CHUNK), jnp.arange(nq) * MOBA_Q_CHUNK))
    return _from_blocks(out)


def _differential_attention(q, k, v, lam, lam_init, g_sub, cos, sin):
    s_len = q.shape[2]
    nq = s_len // Q_BLOCK
    q1 = _apply_rope(q[..., :DIFF_QK_DIM], cos, sin)
    q2 = _apply_rope(q[..., DIFF_QK_DIM:], cos, sin)
    k1 = _apply_rope(k[..., :DIFF_QK_DIM], cos, sin)
    k2 = _apply_rope(k[..., DIFF_QK_DIM:], cos, sin)
    kpos = jnp.arange(s_len)
    scale = DIFF_QK_DIM ** -0.5

    def one_block(args):
        q1b, q2b, q0 = args
        qpos = q0 + jnp.arange(Q_BLOCK)
        causal = kpos[None, :] <= qpos[:, None]
        l1 = jnp.einsum('bhqd,bhkd->bhqk', q1b, k1).astype(jnp.float32) * scale
        l2 = jnp.einsum('bhqd,bhkd->bhqk', q2b, k2).astype(jnp.float32) * scale
        p1 = jax.nn.softmax(jnp.where(causal, l1, -jnp.inf), axis=-1)
        p2 = jax.nn.softmax(jnp.where(causal, l2, -jnp.inf), axis=-1)
        p = p1 - lam * p2
        return jnp.einsum('bhqk,bhkd->bhqd', p.astype(v.dtype), v)

    out = lax.map(one_block, (_to_blocks(q1, Q_BLOCK), _to_blocks(q2, Q_BLOCK), jnp.arange(nq) * Q_BLOCK))
    out = _from_blocks(out)
    return _rmsnorm(out, g_sub, DIFF_SUBLN_EPS) * (1.0 - lam_init)


def _stick_breaking_attention(q, k, v):
    s_len, d = q.shape[2], q.shape[3]
    nq = s_len // Q_BLOCK
    kpos = jnp.arange(s_len)
    scale = d ** -0.5

    def one_block(args):
        qb, q0 = args
        qpos = q0 + jnp.arange(Q_BLOCK)
        z = jnp.einsum('bhqd,bhkd->bhqk', qb, k).astype(jnp.float32) * scale
        strict = kpos[None, :] < qpos[:, None]
        log_beta = jax.nn.log_sigmoid(z)
        log_1m = jnp.where(strict, jax.nn.log_sigmoid(-z), 0.0)
        between = lax.cumsum(log_1m, axis=3, reverse=True) - log_1m
        a = jnp.where(strict, jnp.exp(log_beta + between), 0.0)
        return jnp.einsum('bhqk,bhkd->bhqd', a.astype(v.dtype), v)

    out = lax.map(one_block, (_to_blocks(q, Q_BLOCK), jnp.arange(nq) * Q_BLOCK))
    return _from_blocks(out)


def setup_inputs(seed: int = 0) -> dict:
    key = jax.random.key(seed)
    ks = jax.random.split(key, 16)
    f32 = jnp.float32

    def nrm(k, shape, scale):
        return jax.random.normal(k, shape, f32) * scale

    return {
        'x': nrm(ks[0], (BATCH, SEQ, D_MODEL), 1.0),
        'w_in': nrm(ks[1], (DEPTH, D_MODEL, IN_COLS), D_MODEL ** -0.5),
        'b_fgate': nrm(ks[2], (DEPTH, HEADS_PER_GROUP), 0.1),
        'w_out': nrm(ks[3], (DEPTH, D_MODEL, D_MODEL), D_MODEL ** -0.5),
        'diff_lq1': nrm(ks[4], (DEPTH, DIFF_QK_DIM), 0.1),
        'diff_lk1': nrm(ks[5], (DEPTH, DIFF_QK_DIM), 0.1),
        'diff_lq2': nrm(ks[6], (DEPTH, DIFF_QK_DIM), 0.1),
        'diff_lk2': nrm(ks[7], (DEPTH, DIFF_QK_DIM), 0.1),
        'diff_subln': 1.0 + nrm(ks[8], (DEPTH, HEAD_DIM), 0.02),
        'attn_norm': 1.0 + nrm(ks[9], (DEPTH, D_MODEL), 0.02),
        'w_gate': nrm(ks[10], (DEPTH, D_MODEL, FFN_HIDDEN), D_MODEL ** -0.5),
        'w_up': nrm(ks[11], (DEPTH, D_MODEL, FFN_HIDDEN), D_MODEL ** -0.5),
        'w_down': nrm(ks[12], (DEPTH, FFN_HIDDEN, D_MODEL), FFN_HIDDEN ** -0.5),
        'ffn_norm': 1.0 + nrm(ks[13], (DEPTH, D_MODEL), 0.02),
        'final_norm': 1.0 + nrm(ks[14], (D_MODEL,), 0.02),
    }


def reference(x, w_in, b_fgate, w_out, diff_lq1, diff_lk1, diff_lq2, diff_lk2, diff_subln,
              attn_norm, w_gate, w_up, w_down, ffn_norm, final_norm):
    s_len = x.shape[1]
    cos_full, sin_full = _rope_tables(s_len, HEAD_DIM)
    cos_half, sin_half = _rope_tables(s_len, DIFF_QK_DIM)
    for l in range(DEPTH):
        h = _rmsnorm(x, attn_norm[l])
        z = jnp.einsum('bsd,dc->bsc', h, w_in[l])
        (fq, fk, fv, fg, mq, mk, mv, dq, dk, dv, sq, sk, sv) = _split_cols(z)

        fox = _forgetting_attention(_heads(fq), _heads(fk), _heads(fv),
                                    (fg + b_fgate[l]).transpose(0, 2, 1))

        moba = _moba_attention(_apply_rope(_heads(mq), cos_full, sin_full),
                               _apply_rope(_heads(mk), cos_full, sin_full), _heads(mv))

        lam_init = 0.8 - 0.6 * math.exp(-0.3 * l)
        lam = (jnp.exp(jnp.sum(diff_lq1[l].astype(jnp.float32) * diff_lk1[l].astype(jnp.float32)))
               - jnp.exp(jnp.sum(diff_lq2[l].astype(jnp.float32) * diff_lk2[l].astype(jnp.float32)))
               + lam_init)
        diff = _differential_attention(_heads(dq), _heads(dk), _heads(dv), lam, lam_init,
                                       diff_subln[l], cos_half, sin_half)

        sb = _stick_breaking_attention(_heads(sq), _heads(sk), _heads(sv))

        mixed = jnp.concatenate([_merge(fox), _merge(moba), _merge(diff), _merge(sb)], axis=-1)
        x = x + jnp.einsum('bsc,cd->bsd', mixed, w_out[l])

        h = _rmsnorm(x, ffn_norm[l])
        act = jax.nn.silu(jnp.einsum('bsd,df->bsf', h, w_gate[l])) * jnp.einsum('bsd,df->bsf', h, w_up[l])
        x = x + jnp.einsum('bsf,fd->bsd', act, w_down[l])
    return _rmsnorm(x, final_norm)
```

```python
import math
import numpy as np
import ml_dtypes
import concourse.bass as bass
import concourse.mybir as mybir
from concourse.bass_utils import run_bass_kernel_spmd

F32 = mybir.dt.float32
BF16 = mybir.dt.bfloat16
AF = mybir.ActivationFunctionType
ALU = mybir.AluOpType
AX = mybir.AxisListType

NEG = -30000.0


class Cfg:
    def __init__(self, D=2048, S=2048, HG=4, FF=5632, L=4, NB=4):
        self.D, self.S, self.HG, self.FF, self.L, self.NB = D, S, HG, FF, L, NB
        self.KC = D // 128
        self.NT = S // 128
        self.NQB = S // 512
        self.GW = HG * 128
        self.FC = FF // 128
        self.INC = 12 * self.GW + HG
        self.TBA = min(1024, S)
        self.NHALF = S // self.TBA
        self.PF = 11 if self.FC % 11 == 0 else (4 if self.FC % 4 == 0 else 1)
        self.PF = min(self.PF, self.KC)
        while self.FC % self.PF:
            self.PF -= 1
        self.FG = 4 if self.FC % 4 == 0 else 1
        GW = self.GW
        o = {}
        o['fq'], o['fk'], o['fv'], o['fg'] = 0, GW, 2 * GW, 3 * GW
        b = 3 * GW + HG
        for i, n in enumerate(['mq', 'mk', 'mv', 'dq', 'dk', 'dv', 'sq', 'sk', 'sv']):
            o[n] = b + i * GW
        self.off = o


def host_consts(cfg):
    S = cfg.S
    theta = 10000.0
    t = np.arange(S, dtype=np.float32)[None, :]
    inv = (1.0 / (theta ** (np.arange(0, 128, 2, dtype=np.float32) / 128))).astype(np.float32)
    i = np.arange(128)
    ang = (inv[i % 64][:, None] * t).astype(np.float32)
    cm = np.cos(ang).astype(np.float32)
    sm = (np.sin(ang) * np.where(i < 64, -1.0, 1.0)[:, None]).astype(np.float32)
    invd = (1.0 / (theta ** (np.arange(0, 64, 2, dtype=np.float32) / 64))).astype(np.float32)
    ii = i % 64
    angd = (invd[ii % 32][:, None] * t).astype(np.float32)
    cd = np.cos(angd).astype(np.float32)
    sd = (np.sin(angd) * np.where(ii < 32, -1.0, 1.0)[:, None]).astype(np.float32)
    rope = np.stack([cm, sm, cd, sd], 0).astype(np.float32)
    mats = []
    ident = np.eye(128, dtype=np.float32)
    swm = np.zeros((128, 128), np.float32)
    swm[(i + 64) % 128, i] = 1.0
    swd = np.zeros((128, 128), np.float32)
    swd[(i // 64) * 64 + (ii + 32) % 64, i] = 1.0
    p = np.arange(128)[:, None]
    j = np.arange(128)[None, :]
    mincl = (p <= j).astype(np.float32)
    mstr = (p < j).astype(np.float32)
    ones = np.ones((128, 128), np.float32)
    ntri_incl = -(p >= j).astype(np.float32)
    ntri_excl = -(p < j).astype(np.float32)
    mats = [ident, swm, swd, mincl, mstr, ones, ntri_incl, ntri_excl]
    for kb in range(8):
        m = np.zeros((128, 128), np.float32)
        m[kb, :] = 1.0
        mats.append(m)
    mats.append(NEG * (p > j).astype(np.float32))
    cb = np.concatenate(mats, 1).astype(ml_dtypes.bfloat16)
    cf = np.zeros((128, 16), np.float32)
    r = np.arange(128) % 32
    for jj in range(6):
        cf[:, jj] = (r == jj)
    cf[:, 6] = ((r >= 3) & (r < 6))
    cf[:, 7] = (r < 3)
    cf[:, 8] = 1.0
    onesf = np.ones((128, 128), np.float32)
    return rope, cb, cf, onesf


M_ID, M_SWM, M_SWD, M_INCL, M_STR, M_ONES, M_NTI, M_NTE, M_IND, M_NEGU = 0, 1, 2, 3, 4, 5, 6, 7, 8, 16
NMAT = 17


class _Stop(Exception):
    pass


class Prog:
    ENG = ['pe', 'act', 'dve', 'pool', 'sp']
    DMAQ = ['pool', 'sp']
    NDS = 6

    def __init__(self, nc, sems):
        self.nc = nc
        self.q = {e: [] for e in self.ENG}
        self.sem = sems
        self.cnt = {k: 0 for k in sems}
        self.waited = {e: {} for e in self.ENG}
        self.lastw = {}
        self.readers = {}
        self.dma_i = {e: 0 for e in self.DMAQ}
        self.last_tok = {}

    def _wait(self, eng, toks):
        best = {}
        for (s, v) in toks:
            if v > best.get(s, 0):
                best[s] = v
        for s, v in best.items():
            if self.waited[eng].get(s, 0) >= v:
                continue
            self.waited[eng][s] = v
            self.q[eng].append(('w', s, v))

    def _deps(self, reads, writes):
        toks = []
        for k in reads:
            toks += self.lastw.get(k, [])
        for k in writes:
            toks += self.readers.get(k, [])
            toks += self.lastw.get(k, [])
        return toks

    def _commit(self, tok, reads, writes, wappend=False):
        for k in reads:
            self.readers.setdefault(k, []).append(tok)
        for k in writes:
            if wappend:
                self.lastw.setdefault(k, []).append(tok)
            else:
                self.lastw[k] = [tok]
            self.readers[k] = []

    def _newtok(self, eng):
        if eng in self.DMAQ:
            i = self.dma_i[eng]
            self.dma_i[eng] += 1
            s = '%s%d' % (eng, i % self.NDS)
            if self.cnt[s] > 0:
                self._wait(eng, [(s, self.cnt[s])])
            self.cnt[s] += 16
            return (s, self.cnt[s]), 16
        self.cnt[eng] += 1
        return (eng, self.cnt[eng]), 1

    def op(self, eng, fn, reads=(), writes=(), wappend=False):
        self._wait(eng, self._deps(reads, () if wappend else writes))
        tok, inc = self._newtok(eng)
        self.q[eng].append(('o', fn, tok[0], inc))
        self._commit(tok, reads, writes, wappend)
        self.last_tok[tok[0]] = tok
        return tok

    def group(self, eng, fns, reads=(), writes=()):
        self._wait(eng, self._deps(reads, writes))
        for fn in fns[:-1]:
            self.q[eng].append(('o', fn, None, 0))
        tok, inc = self._newtok(eng)
        self.q[eng].append(('o', fns[-1], tok[0], inc))
        self._commit(tok, reads, writes)
        self.last_tok[tok[0]] = tok
        return tok

    def nop(self, *a, **k):
        return None

    def barrier(self):
        toks = list(self.last_tok.values())
        for e in self.ENG:
            self._wait(e, toks)

    def replay(self, eng, e):
        for it in self.q[eng]:
            if it[0] == 'w':
                e.wait_ge(self.sem[it[1]], it[2])
            else:
                ins = it[1](e)
                if it[2] is not None:
                    ins.then_inc(self.sem[it[2]], it[3])


def build(cfg):
    D, S, HG, FF, L = cfg.D, cfg.S, cfg.HG, cfg.FF, cfg.L
    KC, NT, NQB, GW, FC, TBA, NHALF, PF, FG = cfg.KC, cfg.NT, cfg.NQB, cfg.GW, cfg.FC, cfg.TBA, cfg.NHALF, cfg.PF, cfg.FG
    NTA = TBA // 128
    NBA = TBA // 512
    NH = 4 * HG
    nc = bass.Bass("TRN2", target_bir_lowering=False)

    def din(name, shape, dt=F32):
        return nc.dram_tensor(name, list(shape), dt, kind="ExternalInput").ap()

    x_in = din("x", [S, D])
    w_in = din("w_in", [L, D, cfg.INC])
    w_fg = din("w_fg", [L, D, 128])
    b_rep = din("b_rep", [L, 128, 1])
    w_out = din("w_out", [L, D, D])
    lam_in = din("lam_in", [128, L * 4 * 64])
    gsub = din("gsub", [L, 128, 1])
    gains = din("gains", [2 * L + 1, 128, D])
    w_gate = din("w_gate", [L, D, FF])
    w_up = din("w_up", [L, D, FF])
    w_down = din("w_down", [L, FF, D])
    rope = din("rope", [4, 128, S])
    cb_in = din("cb", [128, NMAT * 128], BF16)
    cf_in = din("cf", [128, 16])
    onesf_in = din("onesf", [128, 128])
    y = nc.dram_tensor("y", [S, D], F32, kind="ExternalOutput").ap()

    xres = nc.dram_tensor("xres", [S, D], F32).ap()
    QTs = nc.dram_tensor("QTs", [NH, 128, S], BF16).ap()
    KTs = nc.dram_tensor("KTs", [NH, 128, S], BF16).ap()
    Vs = nc.dram_tensor("Vs", [4, S, GW], BF16).ap()
    FAs = nc.dram_tensor("FAs", [HG, 6, S], BF16).ap()
    FBs = nc.dram_tensor("FBs", [HG, 6, S], BF16).ap()
    mixT = nc.dram_tensor("mixT", [NH, 128, S], BF16).ap()

    from contextlib import ExitStack
    es = ExitStack()

    def sb(name, shape, dt):
        return es.enter_context(nc.sbuf_tensor(name, list(shape), dt))

    def ps(name, shape, dt):
        return es.enter_context(nc.psum_tensor(name, list(shape), dt))

    NW = 2
    hT = sb("hT", [128, KC, TBA], BF16)
    wsl = [sb("w%d" % i, [128, KC, 512], BF16) for i in range(NW)]
    xt = [sb("xt%d" % i, [128, D], F32) for i in range(2)]
    hb = [sb("hb%d" % i, [128, D], BF16) for i in range(2)]
    gbuf = sb("gbuf", [128, D], F32)
    ropeC = [sb("ropeC%d" % i, [128, 512], F32) for i in range(1)]
    ropeS = [sb("ropeS%d" % i, [128, 512], F32) for i in range(1)]
    cb = sb("cbm", [128, NMAT * 128], BF16)
    cf = sb("cfm", [128, 16], F32)
    onesf = sb("onesf_sb", [128, 512], F32)
    small = sb("small", [128, 64], F32)
    lamb = sb("lamb", [128, 4 * 64], F32)
    ssq = sb("ssq", [128, 32], F32)
    UNI = max(FC * 512 + 2 * 1024 + 4 * 1024, 26 * 1024)
    uni = sb("uni", [128, UNI], BF16)
    wfgb = sb("wfgb", [128, KC, 128], BF16)
    NST = 3
    wst32 = [sb("wst%d" % i, [128, 4, 512], F32) for i in range(NST)]

    class Carver:
        def __init__(self):
            self.o = 0

        def get(self, n, dt):
            k = 2 if dt == F32 else 1
            a = uni[:, self.o:self.o + n * k]
            self.o += n * k
            assert self.o <= UNI, self.o
            return a.bitcast(F32) if dt == F32 else a

    ca = Carver()
    QT = ca.get(S, BF16)
    KT = ca.get(S, BF16)
    Vh = ca.get(NT * 128, BF16)
    X1 = ca.get(S, BF16)
    X2 = ca.get(S, BF16)
    fa = X1
    negmT = X1
    KTn = X1
    fb = X2
    OT = [ca.get(S, BF16) for _ in range(2)]
    Pt = [ca.get(512, BF16) for _ in range(3)]
    spb = [ca.get(512, BF16) for _ in range(2)]
    ef = [ca.get(512, F32) for _ in range(2)]
    rden = ca.get(512, F32)
    o1 = ca.get(512, F32)
    o2 = ca.get(512, F32)
    comb = ca.get(512, F32)
    sq = ca.get(512, F32)
    rstdb = ca.get(512, F32)
    kmf = ca.get(8, F32)
    kmb = ca.get(8, BF16)
    gm = ca.get(8, F32)
    top8 = ca.get(8, F32)
    negm = ca.get(8, F32)
    negmb = ca.get(8, BF16)
    ca = Carver()
    stg = [ca.get(512, BF16) for _ in range(2)]
    qraw = [ca.get(512, BF16) for _ in range(2)]
    q32 = [ca.get(512, F32) for _ in range(2)]
    t1 = [ca.get(512, F32) for _ in range(2)]
    t2 = [ca.get(512, F32) for _ in range(2)]
    fgt = {n: ca.get(512, F32) for n in ['e', 'sp', 'hi32', 'r1', 'mid32', 'r2', 'acc', 'acc2']}
    csp = [ca.get(512, F32) for _ in range(2)]
    fgb = {n: ca.get(512, BF16) for n in ['hi', 'mid', 'fa', 'fb']}
    ca = Carver()
    xa = [ca.get(512, F32) for _ in range(4)]
    cf_ = Carver()
    actT = cf_.get(FC * 512, BF16)
    sg = [cf_.get(512, F32) for _ in range(2)]
    xa2 = [cf_.get(512, F32) for _ in range(4)]

    def actT_ap(f, a, b):
        return actT[:, f * 512 + a: f * 512 + b]

    NPS = 6
    psb = [ps("ps%d" % i, [128, 512], F32) for i in range(NPS)]
    tpb = [ps("tpb%d" % i, [128, 1024], BF16) for i in range(2)]
    tp = [tpb[0][:, 0:512], tpb[1][:, 0:512]]

    semnames = ['pe', 'act', 'dve'] + ['pool%d' % i for i in range(Prog.NDS)] + ['sp%d' % i for i in range(Prog.NDS)]
    sems = {n: es.enter_context(nc.semaphore(n)) for n in semnames}
    P = Prog(nc, sems)

    def cm(i):
        return cb[:, i * 128:(i + 1) * 128]

    def col(t, i):
        return t[:, i:i + 1]

    P.op('pool', lambda e: e.dma_start(out=cb[:], in_=cb_in[:, :]), writes=['cb'])
    P.op('pool', lambda e: e.dma_start(out=cf[:], in_=cf_in[:, :]), writes=['cf'])
    for i in range(4):
        P.op('pool', lambda e, i=i: e.dma_start(out=onesf[:, i * 128:(i + 1) * 128], in_=onesf_in[:, :]), writes=['onesf'])

    wstate = {'i': 0}

    def wload(parts):
        i = wstate['i']
        wstate['i'] += 1
        slot = i % NW
        w = wsl[slot]
        first = True
        for (dstf, src) in parts:
            d = dstf(w)
            n1 = d.shape[1]
            nco = d.shape[2]
            for a in range(0, n1, 4):
                b = min(n1, a + 4)
                st = wstate.get('st', 0)
                wstate['st'] = st + 1
                sg32 = wst32[st % NST]
                P.op('pool', lambda e, o=sg32[:, 0:b - a, 0:nco], s=src[:, a:b, :]: e.dma_start(out=o, in_=s), writes=[('wst', st % NST)])
                P.op('act', lambda e, o=d[:, a:b, :], i_=sg32[:, 0:b - a, 0:nco]: e.copy(out=o, in_=i_),
                     reads=[('wst', st % NST)], writes=[('w', slot)], wappend=not first)
                first = False
        return w, ('w', slot)

    ring = {'i': 0}

    def nextps(n=6, base=0):
        i = ring['i']
        ring['i'] += 1
        k = base + i % n
        return psb[k], ('ps', k)

    def wsrc(wap, l, c0, ncols):
        return wap[l, :, c0:c0 + ncols].rearrange("(kc p) c -> p kc c", p=128)

    def norm_T(xsrc, tok0, ntile, gkey, xkey):
        for tl in range(ntile):
            r0 = tok0 + tl * 128
            b = tl % 2
            P.op('pool', lambda e, b=b, r0=r0: e.dma_start(out=xt[b][:], in_=xsrc[r0:r0 + 128, :]),
                 reads=[(xkey, r0 // 128, c) for c in range(D // 512)], writes=[('xt', b)])
            P.op('act', lambda e, b=b, tl=tl: e.activation(out=hb[1 - b][:], in_=xt[b][:], func=AF.Square,
                                                          accum_out=ssq[:, tl:tl + 1]),
                 reads=[('xt', b)], writes=[('hb', 1 - b), ('ssq', tl)])
            P.op('dve', lambda e, tl=tl: e.tensor_scalar(out=ssq[:, tl:tl + 1], in0=ssq[:, tl:tl + 1], scalar1=1.0 / D,
                                                         scalar2=1e-6, op0=ALU.mult, op1=ALU.add),
                 reads=[('ssq', tl)], writes=[('ssq', tl)])
            P.op('act', lambda e, tl=tl: e.activation(out=ssq[:, tl:tl + 1], in_=ssq[:, tl:tl + 1], func=AF.Ln),
                 reads=[('ssq', tl)], writes=[('ssq', tl)])
            P.op('act', lambda e, tl=tl: e.activation(out=ssq[:, tl:tl + 1], in_=ssq[:, tl:tl + 1], func=AF.Exp, scale=-0.5),
                 reads=[('ssq', tl)], writes=[('ssq', tl)])
            P.op('dve', lambda e, b=b, tl=tl: e.scalar_tensor_tensor(out=hb[b][:], in0=xt[b][:], scalar=ssq[:, tl:tl + 1],
                                                                     in1=gbuf[:], op0=ALU.mult, op1=ALU.mult),
                 reads=[('xt', b), ('ssq', tl), gkey], writes=[('hb', b)])
            for c4 in range(KC // 4 if KC >= 4 else 1):
                nch = min(4, KC)
                tpi = (tl * 4 + c4) % 2
                fns = [lambda e, j=j, c4=c4, tpi=tpi, b=b: e.transpose(out=tp[tpi][:, j * 128:(j + 1) * 128],
                                                                     in_=hb[b][:, (c4 * 4 + j) * 128:(c4 * 4 + j + 1) * 128],
                                                                     identity=cm(M_ID)) for j in range(nch)]
                P.group('pe', fns, reads=[('hb', b), 'cb'], writes=[('tp', tpi)])
                eng = 'act' if c4 % 2 == 0 else 'dve'
                if eng == 'act':
                    f = lambda e, c4=c4, tl=tl, tpi=tpi, nch=nch: e.copy(
                        out=hT[:, c4 * 4:c4 * 4 + nch, tl * 128:(tl + 1) * 128],
                        in_=tp[tpi][:, 0:nch * 128].rearrange("p (c t) -> p c t", t=128))
                else:
                    f = lambda e, c4=c4, tl=tl, tpi=tpi, nch=nch: e.tensor_copy(
                        out=hT[:, c4 * 4:c4 * 4 + nch, tl * 128:(tl + 1) * 128],
                        in_=tp[tpi][:, 0:nch * 128].rearrange("p (c t) -> p c t", t=128))
                P.op(eng, f, reads=[('tp', tpi)], writes=[('hT', tl)])

    def load_gain(idx):
        P.op('pool', lambda e: e.dma_start(out=gbuf[:], in_=gains[idx, :, :]), writes=['g'])
        return 'g'

    groups = ['f', 'm', 'd', 's']
    stop = getattr(cfg, 'stop', None)
    dbg = getattr(cfg, 'dbg', '')
    for l in range(L):
      try:
          if stop == 'p0':
              break
          xsrc = x_in if l == 0 else xres
          xkey = 'xin' if l == 0 else 'xres'
          lam_init = 0.8 - 0.6 * math.exp(-0.3 * l)
          P.op('pool', lambda e, l=l: e.dma_start(out=small[:, 0:1], in_=b_rep[l, :, :]), writes=['small0'])
          P.op('dve', lambda e: e.tensor_scalar(out=small[:, 1:2], in0=small[:, 0:1], scalar1=-1.0, scalar2=None, op0=ALU.mult),
               reads=['small0'], writes=['negb'])
          P.op('pool', lambda e, l=l: e.dma_start(out=small[:, 2:3], in_=gsub[l, :, :]), writes=['small2'])
          P.op('dve', lambda e, li=lam_init: e.tensor_scalar(out=small[:, 3:4], in0=small[:, 2:3], scalar1=1.0 - li, scalar2=None,
                                                             op0=ALU.mult), reads=['small2'], writes=['gsc'])
          P.op('pool', lambda e, l=l: e.dma_start(out=lamb[:], in_=lam_in[:, l * 256:(l + 1) * 256]), writes=['lamb'])
          P.op('dve', lambda e: e.tensor_tensor(out=lamb[:, 0:64], in0=lamb[:, 0:64], in1=lamb[:, 64:128], op=ALU.mult),
               reads=['lamb'], writes=['lamb'])
          P.op('dve', lambda e: e.tensor_tensor(out=lamb[:, 128:192], in0=lamb[:, 128:192], in1=lamb[:, 192:256], op=ALU.mult),
               reads=['lamb'], writes=['lamb'])
          P.op('dve', lambda e: e.reduce_sum(out=small[:, 4:5], in_=lamb[:, 0:64], axis=AX.X), reads=['lamb'], writes=['s4'])
          P.op('dve', lambda e: e.reduce_sum(out=small[:, 5:6], in_=lamb[:, 128:192], axis=AX.X), reads=['lamb'], writes=['s5'])
          P.op('act', lambda e: e.activation(out=small[:, 6:8], in_=small[:, 4:6], func=AF.Exp), reads=['s4', 's5'], writes=['s67'])
          P.op('dve', lambda e: e.tensor_tensor(out=small[:, 8:9], in0=small[:, 7:8], in1=small[:, 6:7], op=ALU.subtract),
               reads=['s67'], writes=['s8'])
          P.op('dve', lambda e, li=lam_init: e.tensor_scalar(out=small[:, 8:9], in0=small[:, 8:9], scalar1=-li, scalar2=None, op0=ALU.add),
               reads=['s8'], writes=['neglam'])
          for a in range(0, KC, 4):
              st = wstate.get('st', 0)
              wstate['st'] = st + 1
              sg32 = wst32[st % NST]
              P.op('pool', lambda e, l=l, a=a, sg32=sg32: e.dma_start(out=sg32[:, 0:4, 0:128],
                                                                    in_=w_fg[l, :, :].rearrange("(kc p) c -> p kc c", p=128)[:, a:a + 4, :]),
                   writes=[('wst', st % NST)])
              P.op('act', lambda e, a=a, sg32=sg32: e.copy(out=wfgb[:, a:a + 4, :], in_=sg32[:, 0:4, 0:128]),
                   reads=[('wst', st % NST)], writes=['wfg'], wappend=(a > 0))

          for half in range(NHALF):
              tok0 = half * TBA
              gk = load_gain(2 * l)
              norm_T(xsrc, tok0, NTA, gk, xkey)
              if stop == 'p1':
                  raise _Stop()
              hTkeys = [('hT', tl) for tl in range(NTA)]
              for gi, g in enumerate(groups):
                  for xi, xn in enumerate(['q', 'k']):
                      c0 = cfg.off[g + xn]
                      w, wk = wload([(lambda w: w[:, :, 0:GW] if GW == 512 else w[:, :, 0:GW], wsrc(w_in, l, c0, GW))])
                      isrope = g in ('m', 'd')
                      scale = (128 ** -0.5 if g != 'd' else 64 ** -0.5) if xn == 'q' else 1.0
                      for nb in range(NBA):
                          if isrope:
                              rb = 0
                          if isrope and 'nodma' not in dbg:
                              ti = 0 if g == 'm' else 2
                              g0 = tok0 + nb * 512
                              P.op('pool', lambda e, rb=rb, ti=ti, g0=g0: e.dma_start(out=ropeC[rb][:], in_=rope[ti, :, g0:g0 + 512]),
                                   writes=[('ropeC', rb)])
                              P.op('pool', lambda e, rb=rb, ti=ti, g0=g0: e.dma_start(out=ropeS[rb][:], in_=rope[ti + 1, :, g0:g0 + 512]),
                                   writes=[('ropeS', rb)])
                          for h in range(HG):
                              hd = gi * HG + h
                              pt, pk = nextps()
                              fns = [lambda e, kc=kc, h=h, nb=nb, pt=pt, w=w: e.matmul(
                                  pt[:, :], w[:, kc, h * 128:(h + 1) * 128], hT[:, kc, nb * 512:(nb + 1) * 512],
                                  start=(kc == 0), stop=(kc == KC - 1)) for kc in range(KC)]
                              P.group('pe', fns, reads=[wk] + hTkeys[nb * 4:(nb + 1) * 4], writes=[pk])
                              si = (nb * HG + h) % 2
                              if not isrope:
                                  P.op('act', lambda e, si=si, pt=pt, sc=scale: e.mul(out=stg[si][:], in_=pt[:, :], mul=sc),
                                       reads=[pk], writes=[('stg', si)])
                              else:
                                  P.op('act', lambda e, si=si, pt=pt: e.copy(out=q32[si][:], in_=pt[:, :]), reads=[pk], writes=[('q32', si)])
                                  P.op('act', lambda e, si=si: e.copy(out=qraw[si][:], in_=q32[si][:]), reads=[('q32', si)], writes=[('qraw', si)])
                                  pt2, pk2 = nextps()
                                  swi = M_SWM if g == 'm' else M_SWD
                                  if 'nosw' not in dbg: P.group('pe', [lambda e, pt2=pt2, si=si, swi=swi: e.matmul(pt2[:, :], cm(swi), qraw[si][:], start=True, stop=True)],
                                          reads=[('qraw', si), 'cb'], writes=[pk2])
                                  (P.nop if 'nodve1' in dbg else P.op)('dve', lambda e, si=si, pt=pt, rb=rb, sc=scale: e.scalar_tensor_tensor(
                                      out=t1[si][:], in0=q32[si][:], scalar=sc, in1=ropeC[rb][:], op0=ALU.mult, op1=ALU.mult),
                                      reads=[('q32', si), ('ropeC', rb)], writes=[('t1', si)])
                                  (P.nop if 'nodve2' in dbg else P.op)('dve', lambda e, si=si, pt2=pt2, rb=rb, sc=scale: e.scalar_tensor_tensor(
                                      out=t2[si][:], in0=pt2[:, :], scalar=sc, in1=ropeS[rb][:], op0=ALU.mult, op1=ALU.mult),
                                      reads=[pk2, ('ropeS', rb)], writes=[('t2', si)])
                                  (P.nop if 'nodve3' in dbg else P.op)('dve', lambda e, si=si: e.tensor_tensor(out=stg[si][:], in0=t1[si][:], in1=t2[si][:], op=ALU.add),
                                       reads=[('t1', si), ('t2', si)], writes=[('stg', si)])
                              dst = QTs if xn == 'q' else KTs
                              g0 = tok0 + nb * 512
                              P.op('pool', lambda e, si=si, dst=dst, hd=hd, g0=g0: e.dma_start(out=dst[hd, :, g0:g0 + 512], in_=stg[si][:]),
                                   reads=[('stg', si)], writes=[('qk', xn, hd, g0 // 512)])
                  if stop == 'p2q':
                      raise _Stop()
                  w, wk = wload([(lambda w: w[:, :, 0:GW], wsrc(w_in, l, cfg.off[g + 'v'], GW))])
                  for tl in range(NTA):
                      pt, pk = nextps()
                      fns = [lambda e, kc=kc, tl=tl, pt=pt, w=w: e.matmul(pt[:, 0:GW], hT[:, kc, tl * 128:(tl + 1) * 128], w[:, kc, 0:GW],
                                                                          start=(kc == 0), stop=(kc == KC - 1)) for kc in range(KC)]
                      P.group('pe', fns, reads=[wk, ('hT', tl)], writes=[pk])
                      si = tl % 2
                      P.op('act', lambda e, si=si, pt=pt: e.copy(out=stg[si][:, 0:GW], in_=pt[:, 0:GW]), reads=[pk], writes=[('stg', si)])
                      r0 = tok0 + tl * 128
                      P.op('pool', lambda e, si=si, gi=gi, r0=r0: e.dma_start(out=Vs[gi, r0:r0 + 128, :], in_=stg[si][:, 0:GW]),
                           reads=[('stg', si)], writes=[('v', gi, r0 // 128)])
                  if stop == 'p2v':
                      raise _Stop()
                  if g == 'f':
                      for nb in range(NBA):
                          gb = half * NBA + nb
                          pt, pk = nextps()
                          fns = [lambda e, kc=kc, nb=nb, pt=pt: e.matmul(pt[:, :], wfgb[:, kc, :], hT[:, kc, nb * 512:(nb + 1) * 512],
                                                                         start=(kc == 0), stop=(kc == KC - 1)) for kc in range(KC)]
                          P.group('pe', fns, reads=['wfg'] + hTkeys[nb * 4:(nb + 1) * 4], writes=[pk])
                          P.op('act', lambda e, pt=pt: e.activation(out=fgt['e'][:], in_=pt[:, :], func=AF.Exp, bias=small[:, 1:2], scale=-1.0),
                               reads=[pk, 'negb'], writes=['fe'])
                          P.op('act', lambda e: e.activation(out=fgt['sp'][:], in_=fgt['e'][:], func=AF.Ln, bias=1.0, scale=1.0),
                               reads=['fe'], writes=['fsp'])
                          cpar = gb % 2
                          if gb == 0:
                              P.op('dve', lambda e, cpar=cpar: e.tensor_tensor_scan(out=csp[cpar][:], data0=onesf[:], data1=fgt['sp'][:],
                                                                                  initial=0.0, op0=ALU.mult, op1=ALU.add),
                                   reads=['fsp', 'onesf'], writes=[('csp', cpar)])
                          else:
                              P.op('dve', lambda e, cpar=cpar: e.tensor_tensor_scan(out=csp[cpar][:], data0=onesf[:], data1=fgt['sp'][:],
                                                                                  initial=csp[1 - cpar][:, 511:512], op0=ALU.mult, op1=ALU.add),
                                   reads=['fsp', 'onesf', ('csp', 1 - cpar)], writes=[('csp', cpar)])
                          c = csp[cpar]
                          seq = [
                              (lambda e, c=c: e.tensor_copy(out=fgb['hi'][:], in_=c[:])),
                              (lambda e: e.tensor_copy(out=fgt['hi32'][:], in_=fgb['hi'][:])),
                              (lambda e, c=c: e.tensor_tensor(out=fgt['r1'][:], in0=c[:], in1=fgt['hi32'][:], op=ALU.subtract)),
                              (lambda e: e.tensor_copy(out=fgb['mid'][:], in_=fgt['r1'][:])),
                              (lambda e: e.tensor_copy(out=fgt['mid32'][:], in_=fgb['mid'][:])),
                              (lambda e: e.tensor_tensor(out=fgt['r2'][:], in0=fgt['r1'][:], in1=fgt['mid32'][:], op=ALU.subtract)),
                              (lambda e: e.tensor_scalar(out=fgt['acc'][:], in0=fgt['hi32'][:], scalar1=col(cf, 0), scalar2=None, op0=ALU.mult)),
                              (lambda e: e.scalar_tensor_tensor(out=fgt['acc'][:], in0=fgt['mid32'][:], scalar=col(cf, 1), in1=fgt['acc'][:],
                                                                op0=ALU.mult, op1=ALU.add)),
                              (lambda e: e.scalar_tensor_tensor(out=fgt['acc'][:], in0=fgt['r2'][:], scalar=col(cf, 2), in1=fgt['acc'][:],
                                                                op0=ALU.mult, op1=ALU.add)),
                              (lambda e: e.tensor_scalar(out=fgb['fa'][:], in0=fgt['acc'][:], scalar1=col(cf, 6), scalar2=None, op0=ALU.add)),
                              (lambda e: e.tensor_scalar(out=fgt['acc2'][:], in0=fgt['hi32'][:], scalar1=col(cf, 3), scalar2=None, op0=ALU.mult)),
                              (lambda e: e.scalar_tensor_tensor(out=fgt['acc2'][:], in0=fgt['mid32'][:], scalar=col(cf, 4), in1=fgt['acc2'][:],
                                                                op0=ALU.mult, op1=ALU.add)),
                              (lambda e: e.scalar_tensor_tensor(out=fgt['acc2'][:], in0=fgt['r2'][:], scalar=col(cf, 5), in1=fgt['acc2'][:],
                                                                op0=ALU.mult, op1=ALU.add)),
                              (lambda e: e.tensor_scalar(out=fgb['fb'][:], in0=fgt['acc2'][:], scalar1=-1.0, scalar2=col(cf, 7), op0=ALU.mult,
                                                         op1=ALU.add)),
                          ]
                          for fn in seq:
                              P.op('dve', fn, reads=['fchain', ('csp', cpar), 'cf'], writes=['fchain'])
                          g0 = tok0 + nb * 512
                          for h in range(HG):
                              P.op('pool', lambda e, h=h, g0=g0: e.dma_start(out=FAs[h, :, g0:g0 + 512], in_=fgb['fa'][h * 32:h * 32 + 6, :]),
                                   reads=['fchain'], writes=[('fa', h, g0 // 512)])
                              P.op('pool', lambda e, h=h, g0=g0: e.dma_start(out=FBs[h, :, g0:g0 + 512], in_=fgb['fb'][h * 32:h * 32 + 6, :]),
                                   reads=['fchain'], writes=[('fbk', h, g0 // 512)])
                  if stop == 'p2' + g:
                      raise _Stop()
          P.barrier()
          if stop == 'p2':
              break

          S0, S1, Ob, Db, Qb, Xb = 0, 1, 2, 3, 4, 5
          for gi, g in enumerate(groups):
              for h in range(HG):
                  hd = gi * HG + h
                  P.op('pool', lambda e, hd=hd: e.dma_start(out=QT[:], in_=QTs[hd, :, :]),
                       reads=[('qk', 'q', hd, b) for b in range(NQB)], writes=['QT'])
                  P.op('pool', lambda e, hd=hd: e.dma_start(out=KT[:], in_=KTs[hd, :, :]),
                       reads=[('qk', 'k', hd, b) for b in range(NQB)], writes=['KT'])
                  for a in range(0, NT, 4):
                      P.op('pool', lambda e, gi=gi, h=h, a=a: e.dma_start(
                          out=Vh[:].rearrange("p (t c) -> p t c", c=128)[:, a:a + 4, :],
                          in_=Vs[gi, :, h * 128:(h + 1) * 128].rearrange("(t p) c -> p t c", p=128)[:, a:a + 4, :]),
                          reads=[('v', gi, t) for t in range(NT)], writes=['Vh'], wappend=(a > 0))
                  if g == 'f':
                      P.op('pool', lambda e, h=h: e.dma_start(out=fa[0:6, :], in_=FAs[h, :, :]),
                           reads=[('fa', h, b) for b in range(NQB)], writes=['fa'])
                      P.op('pool', lambda e, h=h: e.dma_start(out=fb[0:6, :], in_=FBs[h, :, :]),
                           reads=[('fbk', h, b) for b in range(NQB)], writes=['fb'])
                  if g == 's':
                      P.op('dve', lambda e: e.tensor_scalar(out=KTn[:], in0=KT[:], scalar1=-1.0, scalar2=None, op0=ALU.mult),
                           reads=['KT'], writes=['KTn'])
                  if g == 'm':
                      nkb = S // 256
                      P.op('dve', lambda e: e.memset(negmT[0:8, :], 0.0), writes=['negmT'])
                      if nkb > 4:
                          P.op('dve', lambda e, nkb=nkb: e.tensor_reduce(out=kmf[:, 0:nkb], in_=KT[:].rearrange("p (n k) -> p n k", k=256),
                                                                        axis=AX.X, op=ALU.add), reads=['KT'], writes=['kmf'])
                          P.op('dve', lambda e, nkb=nkb: e.tensor_scalar(out=kmb[:, 0:nkb], in0=kmf[:, 0:nkb], scalar1=1.0 / 256, scalar2=None,
                                                                        op0=ALU.mult), reads=['kmf'], writes=['kmb'])
                          P.op('dve', lambda e: e.memset(gm[:], -1e30), writes=['gm'])
                          P.op('dve', lambda e: e.memset(negm[:], 0.0), writes=['negm'])
                          for qt in range(8, NT):
                              ob = qt // 2
                              P.group('pe', [lambda e, qt=qt, nkb=nkb: e.matmul(psb[Qb][:, 0:nkb], QT[:, qt * 128:(qt + 1) * 128], kmb[:, 0:nkb],
                                                                               start=True, stop=True)], reads=['QT', 'kmb'], writes=[('ps', Qb)])
                              P.op('dve', lambda e, ob=ob: e.tensor_copy(out=gm[:, 0:ob], in_=psb[Qb][:, 0:ob]), reads=[('ps', Qb)], writes=['gm'])
                              P.op('dve', lambda e: e.max(out=top8[:], in_=gm[:]), reads=['gm'], writes=['top8'])
                              P.op('dve', lambda e, ob=ob: e.tensor_scalar(out=negm[:, 0:ob], in0=gm[:, 0:ob], scalar1=top8[:, 2:3], scalar2=-NEG,
                                                                          op0=ALU.is_ge, op1=ALU.mult), reads=['gm', 'top8'], writes=['negm'])
                              P.op('dve', lambda e, ob=ob: e.tensor_scalar(out=negm[:, 0:ob], in0=negm[:, 0:ob], scalar1=NEG, scalar2=None,
                                                                          op0=ALU.add), reads=['negm'], writes=['negm'])
                              P.op('dve', lambda e: e.tensor_copy(out=negmb[:], in_=negm[:]), reads=['negm'], writes=['negmb'])
                              P.group('pe', [lambda e: e.transpose(out=tp[0][0:8, 0:128], in_=negmb[:, 0:8], identity=cm(M_ID))],
                                      reads=['negmb', 'cb'], writes=[('tp', 0)])
                              P.op('dve', lambda e, qt=qt: e.tensor_copy(out=negmT[0:8, qt * 128:(qt + 1) * 128], in_=tp[0][0:8, 0:128]),
                                   reads=[('tp', 0)], writes=['negmT'])
                  ot = OT[hd % 2]
                  otk = ('OT', hd % 2)
                  for qb in range(NQB):
                      nkt = 4 * (qb + 1)
                      q0 = qb * 512
                      if g != 's':
                          maps = [(0, 128)] if g != 'd' else [(0, 64), (64, 128)]
                          for mi, (pa, pb) in enumerate(maps):
                              for kt in range(nkt):
                                  c0 = max(0, kt * 128 - q0)
                                  diag = kt * 128 >= q0
                                  sbk = S0 + (kt % 2)
                                  mm = [(psb[sbk][:, c0:512], KT[pa:pb, kt * 128:(kt + 1) * 128], QT[pa:pb, q0 + c0:q0 + 512])]
                                  rd = ['KT', 'QT', 'cb']
                                  if g == 'f':
                                      mm.append((psb[sbk][:, c0:512], fa[0:6, kt * 128:(kt + 1) * 128], fb[0:6, q0 + c0:q0 + 512]))
                                      rd += ['fa', 'fb']
                                  if g == 'm' and qb >= 2:
                                      kb = kt // 2
                                      mm.append((psb[sbk][:, c0:512], cb[0:8, (M_IND + kb) * 128:(M_IND + kb + 1) * 128], negmT[0:8, q0 + c0:q0 + 512]))
                                      rd += ['negmT']
                                  if diag:
                                      mm.append((psb[sbk][:, c0:c0 + 128], cm(M_ID), cm(M_NEGU)))
                                  fns = [lambda e, o=o, a_=a_, b_=b_, i=i, n=len(mm): e.matmul(o, a_, b_, start=(i == 0), stop=(i == n - 1))
                                         for i, (o, a_, b_) in enumerate(mm)]
                                  P.group('pe', fns, reads=rd, writes=[('ps', sbk)])
                                  pi = kt % 3
                                  P.op('act', lambda e, pi=pi, sbk=sbk, c0=c0: e.activation(out=Pt[pi][:, c0:512], in_=psb[sbk][:, c0:512], func=AF.Exp),
                                       reads=[('ps', sbk)], writes=[('P', pi)])
                                  fns = [lambda e, kt=kt, c0=c0, pi=pi, nkt=nkt: e.matmul(psb[Ob][:, c0:512], Vh[:, kt * 128:(kt + 1) * 128], Pt[pi][:, c0:512],
                                                                                         start=(kt == 0), stop=(kt == nkt - 1)),
                                         lambda e, kt=kt, c0=c0, pi=pi, nkt=nkt: e.matmul(psb[Db][:, c0:512], cm(M_ONES), Pt[pi][:, c0:512],
                                                                                         start=(kt == 0), stop=(kt == nkt - 1))]
                                  P.group('pe', fns, reads=[('P', pi), 'Vh', 'cb'], writes=[('ps', Ob), ('ps', Db)])
                              P.op('dve', lambda e: e.reciprocal(out=rden[:], in_=psb[Db][:, :]), reads=[('ps', Db)], writes=['rden'])
                              if g != 'd':
                                  P.op('dve', lambda e, ot=ot, q0=q0: e.tensor_tensor(out=ot[:, q0:q0 + 512], in0=psb[Ob][:, :], in1=rden[:], op=ALU.mult),
                                       reads=[('ps', Ob), 'rden'], writes=[otk])
                              else:
                                  od = o1 if mi == 0 else o2
                                  P.op('dve', lambda e, od=od: e.tensor_tensor(out=od[:], in0=psb[Ob][:, :], in1=rden[:], op=ALU.mult),
                                       reads=[('ps', Ob), 'rden'], writes=['o%d' % mi])
                          if g == 'd':
                              P.op('dve', lambda e: e.scalar_tensor_tensor(out=comb[:], in0=o2[:], scalar=small[:, 8:9], in1=o1[:], op0=ALU.mult, op1=ALU.add),
                                   reads=['o0', 'o1', 'neglam'], writes=['comb'])
                              P.op('act', lambda e: e.activation(out=sq[:], in_=comb[:], func=AF.Square), reads=['comb'], writes=['sq'])
                              P.group('pe', [lambda e: e.matmul(psb[Qb][:, :], onesf[:, 0:128], sq[:], start=True, stop=True)],
                                      reads=['sq', 'onesf'], writes=[('ps', Qb)])
                              P.op('dve', lambda e: e.tensor_scalar(out=rstdb[:], in0=psb[Qb][:, :], scalar1=1.0 / 128, scalar2=1e-5, op0=ALU.mult, op1=ALU.add),
                                   reads=[('ps', Qb)], writes=['rstdb'])
                              P.op('act', lambda e: e.activation(out=rstdb[:], in_=rstdb[:], func=AF.Ln), reads=['rstdb'], writes=['rstdb'])
                              P.op('act', lambda e: e.activation(out=rstdb[:], in_=rstdb[:], func=AF.Exp, scale=-0.5), reads=['rstdb'], writes=['rstdb'])
                              P.op('dve', lambda e, ot=ot, q0=q0: e.scalar_tensor_tensor(out=ot[:, q0:q0 + 512], in0=comb[:], scalar=small[:, 3:4], in1=rstdb[:],
                                                                                         op0=ALU.mult, op1=ALU.mult),
                                   reads=['comb', 'rstdb', 'gsc'], writes=[otk])
                      else:
                          for n, kt in enumerate(range(nkt - 1, -1, -1)):
                              c0 = max(0, kt * 128 - q0)
                              diag = kt * 128 >= q0
                              zb = S0 + (n % 2)
                              ei = n % 2
                              P.group('pe', [lambda e, kt=kt, c0=c0, zb=zb, q0=q0: e.matmul(psb[zb][:, c0:512], KT[:, kt * 128:(kt + 1) * 128],
                                                                                           QT[:, q0 + c0:q0 + 512], start=True, stop=True)],
                                      reads=['KT', 'QT'], writes=[('ps', zb)])
                              P.op('act', lambda e, ei=ei, zb=zb, c0=c0: e.activation(out=ef[ei][:, c0:512], in_=psb[zb][:, c0:512], func=AF.Exp),
                                   reads=[('ps', zb)], writes=[('ef', ei)])
                              P.op('act', lambda e, ei=ei, c0=c0: e.activation(out=spb[ei][:, c0:512], in_=ef[ei][:, c0:512], func=AF.Ln, bias=1.0, scale=1.0),
                                   reads=[('ef', ei)], writes=[('spb', ei)])
                              if diag:
                                  P.op('dve', lambda e, ei=ei, c0=c0: e.tensor_tensor(out=spb[ei][:, c0:c0 + 128], in0=spb[ei][:, c0:c0 + 128],
                                                                                     in1=cm(M_STR), op=ALU.mult),
                                       reads=[('spb', ei), 'cb'], writes=[('spb', ei)])
                              fns = [lambda e, ei=ei, c0=c0, n=n: e.matmul(psb[Xb][:, c0:512], cm(M_NTI), spb[ei][:, c0:512], start=(n == 0), stop=False,
                                                                          skip_group_check=True),
                                     lambda e, kt=kt, c0=c0, q0=q0: e.matmul(psb[Xb][:, c0:512], KT[:, kt * 128:(kt + 1) * 128], QT[:, q0 + c0:q0 + 512],
                                                                             start=False, stop=True, skip_group_check=True)]
                              P.group('pe', fns, reads=[('spb', ei), 'KT', 'QT', 'cb'], writes=[('ps', Xb)])
                              pi = n % 3
                              P.op('act', lambda e, pi=pi, c0=c0: e.activation(out=Pt[pi][:, c0:512], in_=psb[Xb][:, c0:512], func=AF.Exp),
                                   reads=[('ps', Xb)], writes=[('P', pi)])
                              if diag:
                                  P.op('dve', lambda e, pi=pi, c0=c0: e.tensor_tensor(out=Pt[pi][:, c0:c0 + 128], in0=Pt[pi][:, c0:c0 + 128],
                                                                                     in1=cm(M_STR), op=ALU.mult),
                                       reads=[('P', pi), 'cb'], writes=[('P', pi)])
                              fns = []
                              if kt > 0:
                                  fns += [lambda e, kt=kt, c0=c0, q0=q0: e.matmul(psb[Xb][:, c0:512], KTn[:, kt * 128:(kt + 1) * 128], QT[:, q0 + c0:q0 + 512],
                                                                                  start=False, stop=False, skip_group_check=True),
                                          lambda e, ei=ei, c0=c0: e.matmul(psb[Xb][:, c0:512], cm(M_NTE), spb[ei][:, c0:512], start=False, stop=True,
                                                                           skip_group_check=True)]
                              fns.append(lambda e, kt=kt, c0=c0, pi=pi, n=n, nkt=nkt: e.matmul(psb[Ob][:, c0:512], Vh[:, kt * 128:(kt + 1) * 128], Pt[pi][:, c0:512],
                                                                                              start=(n == 0), stop=(n == nkt - 1), skip_group_check=True))
                              P.group('pe', fns, reads=[('P', pi), ('spb', ei), 'KTn', 'QT', 'Vh', 'cb'], writes=[('ps', Xb), ('ps', Ob)])
                          P.op('act', lambda e, ot=ot, q0=q0: e.copy(out=ot[:, q0:q0 + 512], in_=psb[Ob][:, :]), reads=[('ps', Ob)], writes=[otk])
                  P.op('pool', lambda e, ot=ot, hd=hd: e.dma_start(out=mixT[hd, :, :], in_=ot[:]), reads=[otk], writes=[('mix', hd)])
          P.barrier()
          if stop == 'p3':
              break

          for half in range(NHALF):
              tok0 = half * TBA
              for hd in range(NH):
                  P.op('pool', lambda e, hd=hd, tok0=tok0: e.dma_start(out=hT[:, hd, :], in_=mixT[hd, :, tok0:tok0 + TBA]),
                       reads=[('mix', hd)], writes=[('hT', tl) for tl in range(NTA)])
              for c in range(D // 512):
                  w, wk = wload([(lambda w: w[:, :, :], wsrc(w_out, l, c * 512, 512))])
                  for tl in range(NTA):
                      r0 = tok0 + tl * 128
                      pt, pk = nextps()
                      xi = (c * NTA + tl) % 4
                      P.op('pool', lambda e, xi=xi, r0=r0, c=c, xsrc=xsrc: e.dma_start(out=xa[xi][:], in_=xsrc[r0:r0 + 128, c * 512:(c + 1) * 512]),
                           reads=[(xkey, r0 // 128, c)], writes=[('xa', xi)])
                      fns = [lambda e, kc=kc, tl=tl, pt=pt, w=w: e.matmul(pt[:, :], hT[:, kc, tl * 128:(tl + 1) * 128], w[:, kc, :],
                                                                          start=(kc == 0), stop=(kc == KC - 1)) for kc in range(KC)]
                      P.group('pe', fns, reads=[wk, ('hT', tl)], writes=[pk])
                      P.op('dve', lambda e, xi=xi, pt=pt: e.tensor_tensor(out=xa[xi][:], in0=pt[:, :], in1=xa[xi][:], op=ALU.add),
                           reads=[pk, ('xa', xi)], writes=[('xa', xi)])
                      P.op('pool', lambda e, xi=xi, r0=r0, c=c: e.dma_start(out=xres[r0:r0 + 128, c * 512:(c + 1) * 512], in_=xa[xi][:]),
                           reads=[('xa', xi)], writes=[('xres', r0 // 128, c)])
          P.barrier()
          if stop == 'p4':
              break

          gk = load_gain(2 * l + 1)
          for tb in range(S // 512):
              tok0 = tb * 512
              norm_T(xres, tok0, 4, gk, 'xres')
              hkeys = [('hT', tl) for tl in range(4)]
              for fg in range(FC // FG):
                  wg, wgk = wload([(lambda w: w[:, :, 0:FG * 128], wsrc(w_gate, l, fg * FG * 128, FG * 128))])
                  wu, wuk = wload([(lambda w: w[:, :, 0:FG * 128], wsrc(w_up, l, fg * FG * 128, FG * 128))])
                  for fc in range(FG):
                      f = fg * FG + fc
                      pg, pgk = nextps(2)
                      fns = [lambda e, kc=kc, fc=fc, pg=pg, wg=wg: e.matmul(pg[:, :], wg[:, kc, fc * 128:(fc + 1) * 128], hT[:, kc, 0:512],
                                                                            start=(kc == 0), stop=(kc == KC - 1)) for kc in range(KC)]
                      P.group('pe', fns, reads=[wgk] + hkeys, writes=[pgk])
                      pu, puk = nextps(2)
                      fns = [lambda e, kc=kc, fc=fc, pu=pu, wu=wu: e.matmul(pu[:, :], wu[:, kc, fc * 128:(fc + 1) * 128], hT[:, kc, 0:512],
                                                                            start=(kc == 0), stop=(kc == KC - 1)) for kc in range(KC)]
                      P.group('pe', fns, reads=[wuk] + hkeys, writes=[puk])
                      si = f % 2
                      P.op('act', lambda e, si=si, pg=pg: e.activation(out=sg[si][:], in_=pg[:, :], func=AF.Silu), reads=[pgk], writes=[('sg', si)])
                      P.op('dve', lambda e, si=si, pu=pu, f=f: e.tensor_tensor(out=actT_ap(f, 0, 512), in0=sg[si][:], in1=pu[:, :], op=ALU.mult),
                           reads=[('sg', si), puk], writes=[('actT', f)])
              for c in range(D // 512):
                  for pc in range(FC // PF):
                      wd, wdk = wload([(lambda w: w[:, 0:PF, :],
                                        w_down[l, pc * PF * 128:(pc + 1) * PF * 128, c * 512:(c + 1) * 512].rearrange("(f p) c -> p f c", p=128))])
                      fns = []
                      for f in range(PF):
                          fa_ = pc * PF + f
                          for t in range(4):
                              fns.append(lambda e, f=f, fa_=fa_, t=t, wd=wd: e.matmul(psb[2 + t][:, :], actT_ap(fa_, t * 128, (t + 1) * 128), wd[:, f, :],
                                                                                     start=(fa_ == 0), stop=(fa_ == FC - 1), skip_group_check=True))
                      P.group('pe', fns, reads=[wdk] + [('actT', pc * PF + f) for f in range(PF)], writes=[('ps', 2 + t) for t in range(4)])
                  for t in range(4):
                      r0 = tok0 + t * 128
                      xi = t
                      P.op('pool', lambda e, xi=xi, r0=r0, c=c: e.dma_start(out=xa2[xi][:], in_=xres[r0:r0 + 128, c * 512:(c + 1) * 512]),
                           reads=[('xres', r0 // 128, c)], writes=[('xa2', xi)])
                      P.op('dve', lambda e, xi=xi, t=t: e.tensor_tensor(out=xa2[xi][:], in0=psb[2 + t][:, :], in1=xa2[xi][:], op=ALU.add),
                           reads=[('ps', 2 + t), ('xa2', xi)], writes=[('xa2', xi)])
                      P.op('pool', lambda e, xi=xi, r0=r0, c=c: e.dma_start(out=xres[r0:r0 + 128, c * 512:(c + 1) * 512], in_=xa2[xi][:]),
                           reads=[('xa2', xi)], writes=[('xres', r0 // 128, c)])
          P.barrier()
      except _Stop:
        P.barrier()
        break

    gk = load_gain(2 * L)
    for t in range(NT if stop is None else 0):
        r0 = t * 128
        b = t % 2
        P.op('pool', lambda e, b=b, r0=r0: e.dma_start(out=xt[b][:], in_=xres[r0:r0 + 128, :]),
             reads=[('xres', t, c) for c in range(D // 512)], writes=[('xt', b)])
        P.op('act', lambda e, b=b, t=t: e.activation(out=hb[b][:], in_=xt[b][:], func=AF.Square, accum_out=ssq[:, t:t + 1]),
             reads=[('xt', b)], writes=[('hb', b), ('ssq', t)])
        P.op('dve', lambda e, t=t: e.tensor_scalar(out=ssq[:, t:t + 1], in0=ssq[:, t:t + 1], scalar1=1.0 / D, scalar2=1e-6, op0=ALU.mult, op1=ALU.add),
             reads=[('ssq', t)], writes=[('ssq', t)])
        P.op('act', lambda e, t=t: e.activation(out=ssq[:, t:t + 1], in_=ssq[:, t:t + 1], func=AF.Ln),
             reads=[('ssq', t)], writes=[('ssq', t)])
        P.op('act', lambda e, t=t: e.activation(out=ssq[:, t:t + 1], in_=ssq[:, t:t + 1], func=AF.Exp, scale=-0.5),
             reads=[('ssq', t)], writes=[('ssq', t)])
        P.op('dve', lambda e, b=b, t=t: e.scalar_tensor_tensor(out=xt[b][:], in0=xt[b][:], scalar=ssq[:, t:t + 1], in1=gbuf[:], op0=ALU.mult, op1=ALU.mult),
             reads=[('xt', b), ('ssq', t), gk], writes=[('xt', b)])
        P.op('pool', lambda e, b=b, r0=r0: e.dma_start(out=y[r0:r0 + 128, :], in_=xt[b][:]), reads=[('xt', b)], writes=[('y', t)])
    P.barrier()

    with nc.Block() as block:
        @block.tensor
        def _(e):
            P.replay('pe', e)

        @block.scalar
        def _(e):
            P.replay('act', e)

        @block.vector
        def _(e):
            P.replay('dve', e)

        @block.gpsimd
        def _(e):
            P.replay('sp', e)

        @block.sync
        def _(e):
            P.replay('pool', e)
    es.close()
    return nc


def host_inputs(cfg, x, w_in, b_fgate, w_out, diff_lq1, diff_lk1, diff_lq2, diff_lk2, diff_subln,
                attn_norm, w_gate, w_up, w_down, ffn_norm, final_norm):
    L, HG, D = cfg.L, cfg.HG, cfg.D
    f32 = np.float32
    w_in = np.ascontiguousarray(np.asarray(w_in, f32))
    fgc = cfg.off['fg']
    w_fg = np.zeros((L, D, 128), f32)
    b_rep = np.zeros((L, 128, 1), f32)
    for h in range(HG):
        w_fg[:, :, h * 32:(h + 1) * 32] = w_in[:, :, fgc + h:fgc + h + 1]
        b_rep[:, h * 32:(h + 1) * 32, 0] = np.asarray(b_fgate, f32)[:, h][:, None]
    lam = np.stack([np.asarray(a, f32) for a in (diff_lq1, diff_lk1, diff_lq2, diff_lk2)], 1)
    lam_in = np.ascontiguousarray(np.broadcast_to(lam.reshape(1, L * 256), (128, L * 256)))
    gsub = np.ascontiguousarray(np.asarray(diff_subln, f32)[:, :, None])
    gl = []
    for l in range(L):
        gl += [np.asarray(attn_norm, f32)[l], np.asarray(ffn_norm, f32)[l]]
    gl.append(np.asarray(final_norm, f32))
    gains = np.ascontiguousarray(np.broadcast_to(np.stack(gl, 0)[:, None, :], (2 * L + 1, 128, D)))
    rope, cbm, cfm, onesf = host_consts(cfg)
    common = dict(w_in=w_in, w_fg=w_fg, b_rep=b_rep, w_out=np.ascontiguousarray(np.asarray(w_out, f32)), lam_in=lam_in, gsub=gsub,
                  gains=gains, w_gate=np.ascontiguousarray(np.asarray(w_gate, f32)), w_up=np.ascontiguousarray(np.asarray(w_up, f32)),
                  w_down=np.ascontiguousarray(np.asarray(w_down, f32)), rope=rope, cb=cbm, cf=cfm, onesf=onesf)
    x = np.asarray(x, f32)
    return [dict(common, x=np.ascontiguousarray(x[b])) for b in range(x.shape[0])]


def kernel(**inputs):
    cfg = Cfg()
    in_maps = host_inputs(cfg, **inputs)
    nc = build(cfg)
    res = run_bass_kernel_spmd(nc, in_maps, core_ids=list(range(len(in_maps))))
    return np.stack([np.asarray(r["y"], np.float32) for r in res.results], 0)
```

```python
import math
import numpy as np
import ml_dtypes
import concourse.bass as bass
import concourse.mybir as mybir
from concourse.bass_utils import run_bass_kernel_spmd

F32 = mybir.dt.float32
BF16 = mybir.dt.bfloat16
AF = mybir.ActivationFunctionType
ALU = mybir.AluOpType
AX = mybir.AxisListType

NEG = -30000.0


class Cfg:
    def __init__(self, D=2048, S=2048, HG=4, FF=5632, L=4, NB=4):
        self.D, self.S, self.HG, self.FF, self.L, self.NB = D, S, HG, FF, L, NB
        self.KC = D // 128
        self.NT = S // 128
        self.NQB = S // 512
        self.GW = HG * 128
        self.FC = FF // 128
        self.INC = 12 * self.GW + HG
        self.TBA = min(1024, S)
        self.NHALF = S // self.TBA
        self.PF = 11 if self.FC % 11 == 0 else (4 if self.FC % 4 == 0 else 1)
        self.PF = min(self.PF, self.KC)
        while self.FC % self.PF:
            self.PF -= 1
        self.FG = 4 if self.FC % 4 == 0 else 1
        GW = self.GW
        o = {}
        o['fq'], o['fk'], o['fv'], o['fg'] = 0, GW, 2 * GW, 3 * GW
        b = 3 * GW + HG
        for i, n in enumerate(['mq', 'mk', 'mv', 'dq', 'dk', 'dv', 'sq', 'sk', 'sv']):
            o[n] = b + i * GW
        self.off = o


def host_consts(cfg):
    S = cfg.S
    theta = 10000.0
    t = np.arange(S, dtype=np.float32)[None, :]
    inv = (1.0 / (theta ** (np.arange(0, 128, 2, dtype=np.float32) / 128))).astype(np.float32)
    i = np.arange(128)
    ang = (inv[i % 64][:, None] * t).astype(np.float32)
    cm = np.cos(ang).astype(np.float32)
    sm = (np.sin(ang) * np.where(i < 64, -1.0, 1.0)[:, None]).astype(np.float32)
    invd = (1.0 / (theta ** (np.arange(0, 64, 2, dtype=np.float32) / 64))).astype(np.float32)
    ii = i % 64
    angd = (invd[ii % 32][:, None] * t).astype(np.float32)
    cd = np.cos(angd).astype(np.float32)
    sd = (np.sin(angd) * np.where(ii < 32, -1.0, 1.0)[:, None]).astype(np.float32)
    rope = np.stack([cm, sm, cd, sd], 0).astype(np.float32)
    mats = []
    ident = np.eye(128, dtype=np.float32)
    swm = np.zeros((128, 128), np.float32)
    swm[(i + 64) % 128, i] = 1.0
    swd = np.zeros((128, 128), np.float32)
    swd[(i // 64) * 64 + (ii + 32) % 64, i] = 1.0
    p = np.arange(128)[:, None]
    j = np.arange(128)[None, :]
    mincl = (p <= j).astype(np.float32)
    mstr = (p < j).astype(np.float32)
    ones = np.ones((128, 128), np.float32)
    ntri_incl = -(p >= j).astype(np.float32)
    ntri_excl = -(p < j).astype(np.float32)
    mats = [ident, swm, swd, mincl, mstr, ones, ntri_incl, ntri_excl]
    for kb in range(8):
        m = np.zeros((128, 128), np.float32)
        m[kb, :] = 1.0
        mats.append(m)
    mats.append(NEG * (p > j).astype(np.float32))
    cb = np.concatenate(mats, 1).astype(ml_dtypes.bfloat16)
    cf = np.zeros((128, 16), np.float32)
    r = np.arange(128) % 32
    for jj in range(6):
        cf[:, jj] = (r == jj)
    cf[:, 6] = ((r >= 3) & (r < 6))
    cf[:, 7] = (r < 3)
    cf[:, 8] = 1.0
    onesf = np.ones((128, 128), np.float32)
    return rope, cb, cf, onesf


M_ID, M_SWM, M_SWD, M_INCL, M_STR, M_ONES, M_NTI, M_NTE, M_IND, M_NEGU = 0, 1, 2, 3, 4, 5, 6, 7, 8, 16
NMAT = 17


class _Stop(Exception):
    pass


class Prog:
    ENG = ['pe', 'act', 'dve', 'pool', 'sp']
    DMAQ = ['pool', 'sp']
    NDS = 12

    def __init__(self, nc, sems):
        self.nc = nc
        self.q = {e: [] for e in self.ENG}
        self.sem = sems
        self.cnt = {k: 0 for k in sems}
        self.waited = {e: {} for e in self.ENG}
        self.lastw = {}
        self.readers = {}
        self.dma_i = {e: 0 for e in self.DMAQ}
        self.last_tok = {}

    def _wait(self, eng, toks):
        best = {}
        for (s, v) in toks:
            if v > best.get(s, 0):
                best[s] = v
        for s, v in best.items():
            if self.waited[eng].get(s, 0) >= v:
                continue
            self.waited[eng][s] = v
            self.q[eng].append(('w', s, v))

    def _deps(self, reads, writes):
        toks = []
        for k in reads:
            toks += self.lastw.get(k, [])
        for k in writes:
            toks += self.readers.get(k, [])
            toks += self.lastw.get(k, [])
        return toks

    def _commit(self, tok, reads, writes, wappend=False):
        for k in reads:
            self.readers.setdefault(k, []).append(tok)
        for k in writes:
            if wappend:
                self.lastw.setdefault(k, []).append(tok)
            else:
                self.lastw[k] = [tok]
            self.readers[k] = []

    def _newtok(self, eng):
        if eng in self.DMAQ:
            i = self.dma_i[eng]
            self.dma_i[eng] += 1
            s = '%s%d' % (eng, i % self.NDS)
            if self.cnt[s] > 0:
                self._wait(eng, [(s, self.cnt[s])])
            self.cnt[s] += 16
            return (s, self.cnt[s]), 16
        self.cnt[eng] += 1
        return (eng, self.cnt[eng]), 1

    def op(self, eng, fn, reads=(), writes=(), wappend=False):
        self._wait(eng, self._deps(reads, () if wappend else writes))
        tok, inc = self._newtok(eng)
        self.q[eng].append(('o', fn, tok[0], inc))
        self._commit(tok, reads, writes, wappend)
        self.last_tok[tok[0]] = tok
        return tok

    def group(self, eng, fns, reads=(), writes=()):
        self._wait(eng, self._deps(reads, writes))
        for fn in fns[:-1]:
            self.q[eng].append(('o', fn, None, 0))
        tok, inc = self._newtok(eng)
        self.q[eng].append(('o', fns[-1], tok[0], inc))
        self._commit(tok, reads, writes)
        self.last_tok[tok[0]] = tok
        return tok

    def nop(self, *a, **k):
        return None

    def barrier(self):
        toks = list(self.last_tok.values())
        for e in self.ENG:
            self._wait(e, toks)

    def replay(self, eng, e):
        for it in self.q[eng]:
            if it[0] == 'w':
                e.wait_ge(self.sem[it[1]], it[2])
            else:
                ins = it[1](e)
                if it[2] is not None:
                    ins.then_inc(self.sem[it[2]], it[3])


def build(cfg):
    D, S, HG, FF, L = cfg.D, cfg.S, cfg.HG, cfg.FF, cfg.L
    KC, NT, NQB, GW, FC, TBA, NHALF, PF, FG = cfg.KC, cfg.NT, cfg.NQB, cfg.GW, cfg.FC, cfg.TBA, cfg.NHALF, cfg.PF, cfg.FG
    NTA = TBA // 128
    TB5 = TBA
    NT5 = TB5 // 128
    NSUB = TB5 // 512
    ACAP = -(-((FC + 1) // 2) // FG) * FG
    passes = [(0, min(ACAP, FC))] + ([(ACAP, FC)] if FC > ACAP else [])
    NBA = TBA // 512
    NH = 4 * HG
    nc = bass.Bass("TRN2", target_bir_lowering=False)

    def din(name, shape, dt=F32):
        return nc.dram_tensor(name, list(shape), dt, kind="ExternalInput").ap()

    x_in = din("x", [S, D])
    w_in = din("w_in", [L, D, cfg.INC])
    w_fg = din("w_fg", [L, D, 128])
    b_rep = din("b_rep", [L, 128, 1])
    w_out = din("w_out", [L, D, D])
    lam_in = din("lam_in", [128, L * 4 * 64])
    gsub = din("gsub", [L, 128, 1])
    gains = din("gains", [2 * L + 1, 128, D])
    w_gate = din("w_gate", [L, D, FF])
    w_up = din("w_up", [L, D, FF])
    w_down = din("w_down", [L, FF, D])
    rope = din("rope", [4, 128, S])
    cb_in = din("cb", [128, NMAT * 128], BF16)
    cf_in = din("cf", [128, 16])
    onesf_in = din("onesf", [128, 128])
    y = nc.dram_tensor("y", [S, D], F32, kind="ExternalOutput").ap()

    xres = nc.dram_tensor("xres", [S, D], F32).ap()
    QTs = nc.dram_tensor("QTs", [NH, 128, S], BF16).ap()
    KTs = nc.dram_tensor("KTs", [NH, 128, S], BF16).ap()
    Vs = nc.dram_tensor("Vs", [4, S, GW], BF16).ap()
    FAs = nc.dram_tensor("FAs", [HG, 6, S], BF16).ap()
    FBs = nc.dram_tensor("FBs", [HG, 6, S], BF16).ap()
    mixT = nc.dram_tensor("mixT", [NH, 128, S], BF16).ap()

    from contextlib import ExitStack
    es = ExitStack()

    def sb(name, shape, dt):
        return es.enter_context(nc.sbuf_tensor(name, list(shape), dt))

    def ps(name, shape, dt):
        return es.enter_context(nc.psum_tensor(name, list(shape), dt))

    NW = 2
    hT = sb("hT", [128, KC, TBA], BF16)
    wsl = [sb("w%d" % i, [128, KC, 512], BF16) for i in range(NW)]
    xt = [sb("xt%d" % i, [128, D], F32) for i in range(2)]
    hb = [sb("hb%d" % i, [128, D], BF16) for i in range(2)]
    gbuf = sb("gbuf", [128, D], F32)
    ropeC = [sb("ropeC%d" % i, [128, 512], F32) for i in range(1)]
    ropeS = [sb("ropeS%d" % i, [128, 512], F32) for i in range(1)]
    cb = sb("cbm", [128, NMAT * 128], BF16)
    cf = sb("cfm", [128, 16], F32)
    onesf = sb("onesf_sb", [128, 512], F32)
    small = sb("small", [128, 64], F32)
    lamb = sb("lamb", [128, 4 * 64], F32)
    ssq = sb("ssq", [128, 32], F32)
    UNI = max(ACAP * TB5 + 2 * 1024 + 4 * 1024, 26 * 1024)
    uni = sb("uni", [128, UNI], BF16)
    wfgb = sb("wfgb", [128, KC, 128], BF16)
    NST = 3
    wst32 = [sb("wst%d" % i, [128, 4, 512], F32) for i in range(NST)]

    class Carver:
        def __init__(self):
            self.o = 0

        def get(self, n, dt):
            k = 2 if dt == F32 else 1
            a = uni[:, self.o:self.o + n * k]
            self.o += n * k
            assert self.o <= UNI, self.o
            return a.bitcast(F32) if dt == F32 else a

    ca = Carver()
    QT = ca.get(S, BF16)
    KT = ca.get(S, BF16)
    Vh = ca.get(NT * 128, BF16)
    X1 = ca.get(S, BF16)
    X2 = ca.get(S, BF16)
    fa = X1
    negmT = X1
    KTn = X1
    fb = X2
    OT = [ca.get(S, BF16) for _ in range(2)]
    Pt = [ca.get(512, BF16) for _ in range(3)]
    spb = [ca.get(512, BF16) for _ in range(2)]
    ef = [ca.get(512, F32) for _ in range(2)]
    rden = ca.get(512, F32)
    o1 = ca.get(512, F32)
    o2 = ca.get(512, F32)
    comb = ca.get(512, F32)
    sq = ca.get(512, F32)
    rstdb = ca.get(512, F32)
    kmf = ca.get(8, F32)
    kmb = ca.get(8, BF16)
    gm = ca.get(8, F32)
    top8 = ca.get(8, F32)
    negm = ca.get(8, F32)
    negmb = ca.get(8, BF16)
    ca = Carver()
    stg = [ca.get(512, BF16) for _ in range(2)]
    qraw = [ca.get(512, BF16) for _ in range(2)]
    q32 = [ca.get(512, F32) for _ in range(2)]
    t1 = [ca.get(512, F32) for _ in range(2)]
    t2 = [ca.get(512, F32) for _ in range(2)]
    fgt = {n: ca.get(512, F32) for n in ['e', 'sp', 'hi32', 'r1', 'mid32', 'r2', 'acc', 'acc2']}
    csp = [ca.get(512, F32) for _ in range(2)]
    fgb = {n: ca.get(512, BF16) for n in ['hi', 'mid', 'fa', 'fb']}
    ca = Carver()
    xa = [ca.get(512, F32) for _ in range(4)]
    cf_ = Carver()
    actT = cf_.get(ACAP * TB5, BF16)
    sg = [cf_.get(512, F32) for _ in range(2)]
    xa2 = [cf_.get(512, F32) for _ in range(4)]

    def actT_ap(f, a, b):
        return actT[:, f * TB5 + a: f * TB5 + b]

    NPS = 6
    psb = [ps("ps%d" % i, [128, 512], F32) for i in range(NPS)]
    tpb = [ps("tpb%d" % i, [128, 1024], BF16) for i in range(2)]
    tp = [tpb[0][:, 0:512], tpb[1][:, 0:512]]

    semnames = ['pe', 'act', 'dve'] + ['pool%d' % i for i in range(Prog.NDS)] + ['sp%d' % i for i in range(Prog.NDS)]
    sems = {n: es.enter_context(nc.semaphore(n)) for n in semnames}
    P = Prog(nc, sems)

    def cm(i):
        return cb[:, i * 128:(i + 1) * 128]

    def col(t, i):
        return t[:, i:i + 1]

    P.op('pool', lambda e: e.dma_start(out=cb[:], in_=cb_in[:, :]), writes=['cb'])
    P.op('pool', lambda e: e.dma_start(out=cf[:], in_=cf_in[:, :]), writes=['cf'])
    for i in range(4):
        P.op('pool', lambda e, i=i: e.dma_start(out=onesf[:, i * 128:(i + 1) * 128], in_=onesf_in[:, :]), writes=['onesf'])

    wstate = {'i': 0}

    def wload(parts):
        i = wstate['i']
        wstate['i'] += 1
        slot = i % NW
        w = wsl[slot]
        first = True
        for (dstf, src) in parts:
            d = dstf(w)
            n1 = d.shape[1]
            nco = d.shape[2]
            for a in range(0, n1, 4):
                b = min(n1, a + 4)
                st = wstate.get('st', 0)
                wstate['st'] = st + 1
                sg32 = wst32[st % NST]
                P.op('pool', lambda e, o=sg32[:, 0:b - a, 0:nco], s=src[:, a:b, :]: e.dma_start(out=o, in_=s), writes=[('wst', st % NST)])
                P.op('act', lambda e, o=d[:, a:b, :], i_=sg32[:, 0:b - a, 0:nco]: e.copy(out=o, in_=i_),
                     reads=[('wst', st % NST)], writes=[('w', slot)], wappend=not first)
                first = False
        return w, ('w', slot)

    ring = {'i': 0}

    def nextps(n=6, base=0):
        i = ring['i']
        ring['i'] += 1
        k = base + i % n
        return psb[k], ('ps', k)

    def wsrc(wap, l, c0, ncols):
        return wap[l, :, c0:c0 + ncols].rearrange("(kc p) c -> p kc c", p=128)

    def norm_T(xsrc, tok0, ntile, gkey, xkey):
        for tl in range(ntile):
            r0 = tok0 + tl * 128
            b = tl % 2
            P.op('pool', lambda e, b=b, r0=r0: e.dma_start(out=xt[b][:], in_=xsrc[r0:r0 + 128, :]),
                 reads=[(xkey, r0 // 128, c) for c in range(D // 512)], writes=[('xt', b)])
            P.op('act', lambda e, b=b, tl=tl: e.activation(out=hb[1 - b][:], in_=xt[b][:], func=AF.Square,
                                                          accum_out=ssq[:, tl:tl + 1]),
                 reads=[('xt', b)], writes=[('hb', 1 - b), ('ssq', tl)])
            P.op('dve', lambda e, tl=tl: e.tensor_scalar(out=ssq[:, tl:tl + 1], in0=ssq[:, tl:tl + 1], scalar1=1.0 / D,
                                                         scalar2=1e-6, op0=ALU.mult, op1=ALU.add),
                 reads=[('ssq', tl)], writes=[('ssq', tl)])
            P.op('act', lambda e, tl=tl: e.activation(out=ssq[:, tl:tl + 1], in_=ssq[:, tl:tl + 1], func=AF.Ln),
                 reads=[('ssq', tl)], writes=[('ssq', tl)])
            P.op('act', lambda e, tl=tl: e.activation(out=ssq[:, tl:tl + 1], in_=ssq[:, tl:tl + 1], func=AF.Exp, scale=-0.5),
                 reads=[('ssq', tl)], writes=[('ssq', tl)])
            P.op('dve', lambda e, b=b, tl=tl: e.scalar_tensor_tensor(out=hb[b][:], in0=xt[b][:], scalar=ssq[:, tl:tl + 1],
                                                                     in1=gbuf[:], op0=ALU.mult, op1=ALU.mult),
                 reads=[('xt', b), ('ssq', tl), gkey], writes=[('hb', b)])
            for c4 in range(KC // 4 if KC >= 4 else 1):
                nch = min(4, KC)
                tpi = (tl * 4 + c4) % 2
                fns = [lambda e, j=j, c4=c4, tpi=tpi, b=b: e.transpose(out=tp[tpi][:, j * 128:(j + 1) * 128],
                                                                     in_=hb[b][:, (c4 * 4 + j) * 128:(c4 * 4 + j + 1) * 128],
                                                                     identity=cm(M_ID)) for j in range(nch)]
                P.group('pe', fns, reads=[('hb', b), 'cb'], writes=[('tp', tpi)])
                eng = 'act' if c4 % 2 == 0 else 'dve'
                if eng == 'act':
                    f = lambda e, c4=c4, tl=tl, tpi=tpi, nch=nch: e.copy(
                        out=hT[:, c4 * 4:c4 * 4 + nch, tl * 128:(tl + 1) * 128],
                        in_=tp[tpi][:, 0:nch * 128].rearrange("p (c t) -> p c t", t=128))
                else:
                    f = lambda e, c4=c4, tl=tl, tpi=tpi, nch=nch: e.tensor_copy(
                        out=hT[:, c4 * 4:c4 * 4 + nch, tl * 128:(tl + 1) * 128],
                        in_=tp[tpi][:, 0:nch * 128].rearrange("p (c t) -> p c t", t=128))
                P.op(eng, f, reads=[('tp', tpi)], writes=[('hT', tl)])

    def load_gain(idx):
        P.op('pool', lambda e: e.dma_start(out=gbuf[:], in_=gains[idx, :, :]), writes=['g'])
        return 'g'

    groups = ['f', 'm', 'd', 's']
    stop = getattr(cfg, 'stop', None)
    dbg = getattr(cfg, 'dbg', '')
    for l in range(L):
      try:
          if stop == 'p0':
              break
          xsrc = x_in if l == 0 else xres
          xkey = 'xin' if l == 0 else 'xres'
          lam_init = 0.8 - 0.6 * math.exp(-0.3 * l)
          P.op('pool', lambda e, l=l: e.dma_start(out=small[:, 0:1], in_=b_rep[l, :, :]), writes=['small0'])
          P.op('dve', lambda e: e.tensor_scalar(out=small[:, 1:2], in0=small[:, 0:1], scalar1=-1.0, scalar2=None, op0=ALU.mult),
               reads=['small0'], writes=['negb'])
          P.op('pool', lambda e, l=l: e.dma_start(out=small[:, 2:3], in_=gsub[l, :, :]), writes=['small2'])
          P.op('dve', lambda e, li=lam_init: e.tensor_scalar(out=small[:, 3:4], in0=small[:, 2:3], scalar1=1.0 - li, scalar2=None,
                                                             op0=ALU.mult), reads=['small2'], writes=['gsc'])
          P.op('pool', lambda e, l=l: e.dma_start(out=lamb[:], in_=lam_in[:, l * 256:(l + 1) * 256]), writes=['lamb'])
          P.op('dve', lambda e: e.tensor_tensor(out=lamb[:, 0:64], in0=lamb[:, 0:64], in1=lamb[:, 64:128], op=ALU.mult),
               reads=['lamb'], writes=['lamb'])
          P.op('dve', lambda e: e.tensor_tensor(out=lamb[:, 128:192], in0=lamb[:, 128:192], in1=lamb[:, 192:256], op=ALU.mult),
               reads=['lamb'], writes=['lamb'])
          P.op('dve', lambda e: e.reduce_sum(out=small[:, 4:5], in_=lamb[:, 0:64], axis=AX.X), reads=['lamb'], writes=['s4'])
          P.op('dve', lambda e: e.reduce_sum(out=small[:, 5:6], in_=lamb[:, 128:192], axis=AX.X), reads=['lamb'], writes=['s5'])
          P.op('act', lambda e: e.activation(out=small[:, 6:8], in_=small[:, 4:6], func=AF.Exp), reads=['s4', 's5'], writes=['s67'])
          P.op('dve', lambda e: e.tensor_tensor(out=small[:, 8:9], in0=small[:, 7:8], in1=small[:, 6:7], op=ALU.subtract),
               reads=['s67'], writes=['s8'])
          P.op('dve', lambda e, li=lam_init: e.tensor_scalar(out=small[:, 8:9], in0=small[:, 8:9], scalar1=-li, scalar2=None, op0=ALU.add),
               reads=['s8'], writes=['neglam'])
          for a in range(0, KC, 4):
              st = wstate.get('st', 0)
              wstate['st'] = st + 1
              sg32 = wst32[st % NST]
              P.op('pool', lambda e, l=l, a=a, sg32=sg32: e.dma_start(out=sg32[:, 0:4, 0:128],
                                                                    in_=w_fg[l, :, :].rearrange("(kc p) c -> p kc c", p=128)[:, a:a + 4, :]),
                   writes=[('wst', st % NST)])
              P.op('act', lambda e, a=a, sg32=sg32: e.copy(out=wfgb[:, a:a + 4, :], in_=sg32[:, 0:4, 0:128]),
                   reads=[('wst', st % NST)], writes=['wfg'], wappend=(a > 0))

          for half in range(NHALF):
              tok0 = half * TBA
              gk = load_gain(2 * l)
              norm_T(xsrc, tok0, NTA, gk, xkey)
              if stop == 'p1':
                  raise _Stop()
              hTkeys = [('hT', tl) for tl in range(NTA)]
              for gi, g in enumerate(groups):
                  for xi, xn in enumerate(['q', 'k']):
                      c0 = cfg.off[g + xn]
                      w, wk = wload([(lambda w: w[:, :, 0:GW] if GW == 512 else w[:, :, 0:GW], wsrc(w_in, l, c0, GW))])
                      isrope = g in ('m', 'd')
                      scale = (128 ** -0.5 if g != 'd' else 64 ** -0.5) if xn == 'q' else 1.0
                      for nb in range(NBA):
                          if isrope:
                              rb = 0
                          if isrope and 'nodma' not in dbg:
                              ti = 0 if g == 'm' else 2
                              g0 = tok0 + nb * 512
                              P.op('pool', lambda e, rb=rb, ti=ti, g0=g0: e.dma_start(out=ropeC[rb][:], in_=rope[ti, :, g0:g0 + 512]),
                                   writes=[('ropeC', rb)])
                              P.op('pool', lambda e, rb=rb, ti=ti, g0=g0: e.dma_start(out=ropeS[rb][:], in_=rope[ti + 1, :, g0:g0 + 512]),
                                   writes=[('ropeS', rb)])
                          for h in range(HG):
                              hd = gi * HG + h
                              pt, pk = nextps()
                              fns = [lambda e, kc=kc, h=h, nb=nb, pt=pt, w=w: e.matmul(
                                  pt[:, :], w[:, kc, h * 128:(h + 1) * 128], hT[:, kc, nb * 512:(nb + 1) * 512],
                                  start=(kc == 0), stop=(kc == KC - 1)) for kc in range(KC)]
                              P.group('pe', fns, reads=[wk] + hTkeys[nb * 4:(nb + 1) * 4], writes=[pk])
                              si = (nb * HG + h) % 2
                              if not isrope:
                                  P.op('act', lambda e, si=si, pt=pt, sc=scale: e.mul(out=stg[si][:], in_=pt[:, :], mul=sc),
                                       reads=[pk], writes=[('stg', si)])
                              else:
                                  P.op('act', lambda e, si=si, pt=pt: e.copy(out=q32[si][:], in_=pt[:, :]), reads=[pk], writes=[('q32', si)])
                                  P.op('act', lambda e, si=si: e.copy(out=qraw[si][:], in_=q32[si][:]), reads=[('q32', si)], writes=[('qraw', si)])
                                  pt2, pk2 = nextps()
                                  swi = M_SWM if g == 'm' else M_SWD
                                  if 'nosw' not in dbg: P.group('pe', [lambda e, pt2=pt2, si=si, swi=swi: e.matmul(pt2[:, :], cm(swi), qraw[si][:], start=True, stop=True)],
                                          reads=[('qraw', si), 'cb'], writes=[pk2])
                                  (P.nop if 'nodve1' in dbg else P.op)('dve', lambda e, si=si, pt=pt, rb=rb, sc=scale: e.scalar_tensor_tensor(
                                      out=t1[si][:], in0=q32[si][:], scalar=sc, in1=ropeC[rb][:], op0=ALU.mult, op1=ALU.mult),
                                      reads=[('q32', si), ('ropeC', rb)], writes=[('t1', si)])
                                  (P.nop if 'nodve2' in dbg else P.op)('dve', lambda e, si=si, pt2=pt2, rb=rb, sc=scale: e.scalar_tensor_tensor(
                                      out=t2[si][:], in0=pt2[:, :], scalar=sc, in1=ropeS[rb][:], op0=ALU.mult, op1=ALU.mult),
                                      reads=[pk2, ('ropeS', rb)], writes=[('t2', si)])
                                  (P.nop if 'nodve3' in dbg else P.op)('dve', lambda e, si=si: e.tensor_tensor(out=stg[si][:], in0=t1[si][:], in1=t2[si][:], op=ALU.add),
                                       reads=[('t1', si), ('t2', si)], writes=[('stg', si)])
                              dst = QTs if xn == 'q' else KTs
                              g0 = tok0 + nb * 512
                              P.op('pool', lambda e, si=si, dst=dst, hd=hd, g0=g0: e.dma_start(out=dst[hd, :, g0:g0 + 512], in_=stg[si][:]),
                                   reads=[('stg', si)], writes=[('qk', xn, hd, g0 // 512)])
                  if stop == 'p2q':
                      raise _Stop()
                  w, wk = wload([(lambda w: w[:, :, 0:GW], wsrc(w_in, l, cfg.off[g + 'v'], GW))])
                  for tl in range(NTA):
                      pt, pk = nextps()
                      fns = [lambda e, kc=kc, tl=tl, pt=pt, w=w: e.matmul(pt[:, 0:GW], hT[:, kc, tl * 128:(tl + 1) * 128], w[:, kc, 0:GW],
                                                                          start=(kc == 0), stop=(kc == KC - 1)) for kc in range(KC)]
                      P.group('pe', fns, reads=[wk, ('hT', tl)], writes=[pk])
                      si = tl % 2
                      P.op('act', lambda e, si=si, pt=pt: e.copy(out=stg[si][:, 0:GW], in_=pt[:, 0:GW]), reads=[pk], writes=[('stg', si)])
                      r0 = tok0 + tl * 128
                      P.op('pool', lambda e, si=si, gi=gi, r0=r0: e.dma_start(out=Vs[gi, r0:r0 + 128, :], in_=stg[si][:, 0:GW]),
                           reads=[('stg', si)], writes=[('v', gi, r0 // 128)])
                  if stop == 'p2v':
                      raise _Stop()
                  if g == 'f':
                      for nb in range(NBA):
                          gb = half * NBA + nb
                          pt, pk = nextps()
                          fns = [lambda e, kc=kc, nb=nb, pt=pt: e.matmul(pt[:, :], wfgb[:, kc, :], hT[:, kc, nb * 512:(nb + 1) * 512],
                                                                         start=(kc == 0), stop=(kc == KC - 1)) for kc in range(KC)]
                          P.group('pe', fns, reads=['wfg'] + hTkeys[nb * 4:(nb + 1) * 4], writes=[pk])
                          P.op('act', lambda e, pt=pt: e.activation(out=fgt['e'][:], in_=pt[:, :], func=AF.Exp, bias=small[:, 1:2], scale=-1.0),
                               reads=[pk, 'negb'], writes=['fe'])
                          P.op('act', lambda e: e.activation(out=fgt['sp'][:], in_=fgt['e'][:], func=AF.Ln, bias=1.0, scale=1.0),
                               reads=['fe'], writes=['fsp'])
                          cpar = gb % 2
                          if gb == 0:
                              P.op('dve', lambda e, cpar=cpar: e.tensor_tensor_scan(out=csp[cpar][:], data0=onesf[:], data1=fgt['sp'][:],
                                                                                  initial=0.0, op0=ALU.mult, op1=ALU.add),
                                   reads=['fsp', 'onesf'], writes=[('csp', cpar)])
                          else:
                              P.op('dve', lambda e, cpar=cpar: e.tensor_tensor_scan(out=csp[cpar][:], data0=onesf[:], data1=fgt['sp'][:],
                                                                                  initial=csp[1 - cpar][:, 511:512], op0=ALU.mult, op1=ALU.add),
                                   reads=['fsp', 'onesf', ('csp', 1 - cpar)], writes=[('csp', cpar)])
                          c = csp[cpar]
                          seq = [
                              (lambda e, c=c: e.tensor_copy(out=fgb['hi'][:], in_=c[:])),
                              (lambda e: e.tensor_copy(out=fgt['hi32'][:], in_=fgb['hi'][:])),
                              (lambda e, c=c: e.tensor_tensor(out=fgt['r1'][:], in0=c[:], in1=fgt['hi32'][:], op=ALU.subtract)),
                              (lambda e: e.tensor_copy(out=fgb['mid'][:], in_=fgt['r1'][:])),
                              (lambda e: e.tensor_copy(out=fgt['mid32'][:], in_=fgb['mid'][:])),
                              (lambda e: e.tensor_tensor(out=fgt['r2'][:], in0=fgt['r1'][:], in1=fgt['mid32'][:], op=ALU.subtract)),
                              (lambda e: e.tensor_scalar(out=fgt['acc'][:], in0=fgt['hi32'][:], scalar1=col(cf, 0), scalar2=None, op0=ALU.mult)),
                              (lambda e: e.scalar_tensor_tensor(out=fgt['acc'][:], in0=fgt['mid32'][:], scalar=col(cf, 1), in1=fgt['acc'][:],
                                                                op0=ALU.mult, op1=ALU.add)),
                              (lambda e: e.scalar_tensor_tensor(out=fgt['acc'][:], in0=fgt['r2'][:], scalar=col(cf, 2), in1=fgt['acc'][:],
                                                                op0=ALU.mult, op1=ALU.add)),
                              (lambda e: e.tensor_scalar(out=fgb['fa'][:], in0=fgt['acc'][:], scalar1=col(cf, 6), scalar2=None, op0=ALU.add)),
                              (lambda e: e.tensor_scalar(out=fgt['acc2'][:], in0=fgt['hi32'][:], scalar1=col(cf, 3), scalar2=None, op0=ALU.mult)),
                              (lambda e: e.scalar_tensor_tensor(out=fgt['acc2'][:], in0=fgt['mid32'][:], scalar=col(cf, 4), in1=fgt['acc2'][:],
                                                                op0=ALU.mult, op1=ALU.add)),
                              (lambda e: e.scalar_tensor_tensor(out=fgt['acc2'][:], in0=fgt['r2'][:], scalar=col(cf, 5), in1=fgt['acc2'][:],
                                                                op0=ALU.mult, op1=ALU.add)),
                              (lambda e: e.tensor_scalar(out=fgb['fb'][:], in0=fgt['acc2'][:], scalar1=-1.0, scalar2=col(cf, 7), op0=ALU.mult,
                                                         op1=ALU.add)),
                          ]
                          for fn in seq:
                              P.op('dve', fn, reads=['fchain', ('csp', cpar), 'cf'], writes=['fchain'])
                          g0 = tok0 + nb * 512
                          for h in range(HG):
                              P.op('pool', lambda e, h=h, g0=g0: e.dma_start(out=FAs[h, :, g0:g0 + 512], in_=fgb['fa'][h * 32:h * 32 + 6, :]),
                                   reads=['fchain'], writes=[('fa', h, g0 // 512)])
                              P.op('pool', lambda e, h=h, g0=g0: e.dma_start(out=FBs[h, :, g0:g0 + 512], in_=fgb['fb'][h * 32:h * 32 + 6, :]),
                                   reads=['fchain'], writes=[('fbk', h, g0 // 512)])
                  if stop == 'p2' + g:
                      raise _Stop()
          P.barrier()
          if stop == 'p2':
              break

          S0, S1, Ob, Db, Qb, Xb = 0, 1, 2, 3, 4, 5
          for gi, g in enumerate(groups):
              for h in range(HG):
                  hd = gi * HG + h
                  P.op('pool', lambda e, hd=hd: e.dma_start(out=QT[:], in_=QTs[hd, :, :]),
                       reads=[('qk', 'q', hd, b) for b in range(NQB)], writes=['QT'])
                  P.op('pool', lambda e, hd=hd: e.dma_start(out=KT[:], in_=KTs[hd, :, :]),
                       reads=[('qk', 'k', hd, b) for b in range(NQB)], writes=['KT'])
                  for a in range(0, NT, 4):
                      P.op('pool', lambda e, gi=gi, h=h, a=a: e.dma_start(
                          out=Vh[:].rearrange("p (t c) -> p t c", c=128)[:, a:a + 4, :],
                          in_=Vs[gi, :, h * 128:(h + 1) * 128].rearrange("(t p) c -> p t c", p=128)[:, a:a + 4, :]),
                          reads=[('v', gi, t) for t in range(NT)], writes=['Vh'], wappend=(a > 0))
                  if g == 'f':
                      P.op('pool', lambda e, h=h: e.dma_start(out=fa[0:6, :], in_=FAs[h, :, :]),
                           reads=[('fa', h, b) for b in range(NQB)], writes=['fa'])
                      P.op('pool', lambda e, h=h: e.dma_start(out=fb[0:6, :], in_=FBs[h, :, :]),
                           reads=[('fbk', h, b) for b in range(NQB)], writes=['fb'])
                  if g == 's':
                      P.op('dve', lambda e: e.tensor_scalar(out=KTn[:], in0=KT[:], scalar1=-1.0, scalar2=None, op0=ALU.mult),
                           reads=['KT'], writes=['KTn'])
                  if g == 'm':
                      nkb = S // 256
                      P.op('dve', lambda e: e.memset(negmT[0:8, :], 0.0), writes=['negmT'])
                      if nkb > 4:
                          P.op('dve', lambda e, nkb=nkb: e.tensor_reduce(out=kmf[:, 0:nkb], in_=KT[:].rearrange("p (n k) -> p n k", k=256),
                                                                        axis=AX.X, op=ALU.add), reads=['KT'], writes=['kmf'])
                          P.op('dve', lambda e, nkb=nkb: e.tensor_scalar(out=kmb[:, 0:nkb], in0=kmf[:, 0:nkb], scalar1=1.0 / 256, scalar2=None,
                                                                        op0=ALU.mult), reads=['kmf'], writes=['kmb'])
                          P.op('dve', lambda e: e.memset(gm[:], -1e30), writes=['gm'])
                          P.op('dve', lambda e: e.memset(negm[:], 0.0), writes=['negm'])
                          for qt in range(8, NT):
                              ob = qt // 2
                              P.group('pe', [lambda e, qt=qt, nkb=nkb: e.matmul(psb[Qb][:, 0:nkb], QT[:, qt * 128:(qt + 1) * 128], kmb[:, 0:nkb],
                                                                               start=True, stop=True)], reads=['QT', 'kmb'], writes=[('ps', Qb)])
                              P.op('dve', lambda e, ob=ob: e.tensor_copy(out=gm[:, 0:ob], in_=psb[Qb][:, 0:ob]), reads=[('ps', Qb)], writes=['gm'])
                              P.op('dve', lambda e: e.max(out=top8[:], in_=gm[:]), reads=['gm'], writes=['top8'])
                              P.op('dve', lambda e, ob=ob: e.tensor_scalar(out=negm[:, 0:ob], in0=gm[:, 0:ob], scalar1=top8[:, 2:3], scalar2=-NEG,
                                                                          op0=ALU.is_ge, op1=ALU.mult), reads=['gm', 'top8'], writes=['negm'])
                              P.op('dve', lambda e, ob=ob: e.tensor_scalar(out=negm[:, 0:ob], in0=negm[:, 0:ob], scalar1=NEG, scalar2=None,
                                                                          op0=ALU.add), reads=['negm'], writes=['negm'])
                              P.op('dve', lambda e: e.tensor_copy(out=negmb[:], in_=negm[:]), reads=['negm'], writes=['negmb'])
                              P.group('pe', [lambda e: e.transpose(out=tp[0][0:8, 0:128], in_=negmb[:, 0:8], identity=cm(M_ID))],
                                      reads=['negmb', 'cb'], writes=[('tp', 0)])
                              P.op('dve', lambda e, qt=qt: e.tensor_copy(out=negmT[0:8, qt * 128:(qt + 1) * 128], in_=tp[0][0:8, 0:128]),
                                   reads=[('tp', 0)], writes=['negmT'])
                  ot = OT[hd % 2]
                  otk = ('OT', hd % 2)
                  for qb in range(NQB):
                      nkt = 4 * (qb + 1)
                      q0 = qb * 512
                      if g != 's':
                          maps = [(0, 128)] if g != 'd' else [(0, 64), (64, 128)]
                          for mi, (pa, pb) in enumerate(maps):
                              for kt in range(nkt):
                                  c0 = max(0, kt * 128 - q0)
                                  diag = kt * 128 >= q0
                                  sbk = S0 + (kt % 2)
                                  mm = [(psb[sbk][:, c0:512], KT[pa:pb, kt * 128:(kt + 1) * 128], QT[pa:pb, q0 + c0:q0 + 512])]
                                  rd = ['KT', 'QT', 'cb']
                                  if g == 'f':
                                      mm.append((psb[sbk][:, c0:512], fa[0:6, kt * 128:(kt + 1) * 128], fb[0:6, q0 + c0:q0 + 512]))
                                      rd += ['fa', 'fb']
                                  if g == 'm' and qb >= 2:
                                      kb = kt // 2
                                      mm.append((psb[sbk][:, c0:512], cb[0:8, (M_IND + kb) * 128:(M_IND + kb + 1) * 128], negmT[0:8, q0 + c0:q0 + 512]))
                                      rd += ['negmT']
                                  if diag:
                                      mm.append((psb[sbk][:, c0:c0 + 128], cm(M_ID), cm(M_NEGU)))
                                  fns = [lambda e, o=o, a_=a_, b_=b_, i=i, n=len(mm): e.matmul(o, a_, b_, start=(i == 0), stop=(i == n - 1))
                                         for i, (o, a_, b_) in enumerate(mm)]
                                  P.group('pe', fns, reads=rd, writes=[('ps', sbk)])
                                  pi = kt % 3
                                  P.op('act', lambda e, pi=pi, sbk=sbk, c0=c0: e.activation(out=Pt[pi][:, c0:512], in_=psb[sbk][:, c0:512], func=AF.Exp),
                                       reads=[('ps', sbk)], writes=[('P', pi)])
                                  fns = [lambda e, kt=kt, c0=c0, pi=pi, nkt=nkt: e.matmul(psb[Ob][:, c0:512], Vh[:, kt * 128:(kt + 1) * 128], Pt[pi][:, c0:512],
                                                                                         start=(kt == 0), stop=(kt == nkt - 1)),
                                         lambda e, kt=kt, c0=c0, pi=pi, nkt=nkt: e.matmul(psb[Db][:, c0:512], cm(M_ONES), Pt[pi][:, c0:512],
                                                                                         start=(kt == 0), stop=(kt == nkt - 1))]
                                  P.group('pe', fns, reads=[('P', pi), 'Vh', 'cb'], writes=[('ps', Ob), ('ps', Db)])
                              P.op('dve', lambda e: e.reciprocal(out=rden[:], in_=psb[Db][:, :]), reads=[('ps', Db)], writes=['rden'])
                              if g != 'd':
                                  P.op('dve', lambda e, ot=ot, q0=q0: e.tensor_tensor(out=ot[:, q0:q0 + 512], in0=psb[Ob][:, :], in1=rden[:], op=ALU.mult),
                                       reads=[('ps', Ob), 'rden'], writes=[otk])
                              else:
                                  od = o1 if mi == 0 else o2
                                  P.op('dve', lambda e, od=od: e.tensor_tensor(out=od[:], in0=psb[Ob][:, :], in1=rden[:], op=ALU.mult),
                                       reads=[('ps', Ob), 'rden'], writes=['o%d' % mi])
                          if g == 'd':
                              P.op('dve', lambda e: e.scalar_tensor_tensor(out=comb[:], in0=o2[:], scalar=small[:, 8:9], in1=o1[:], op0=ALU.mult, op1=ALU.add),
                                   reads=['o0', 'o1', 'neglam'], writes=['comb'])
                              P.op('act', lambda e: e.activation(out=sq[:], in_=comb[:], func=AF.Square), reads=['comb'], writes=['sq'])
                              P.group('pe', [lambda e: e.matmul(psb[Qb][:, :], onesf[:, 0:128], sq[:], start=True, stop=True)],
                                      reads=['sq', 'onesf'], writes=[('ps', Qb)])
                              P.op('dve', lambda e: e.tensor_scalar(out=rstdb[:], in0=psb[Qb][:, :], scalar1=1.0 / 128, scalar2=1e-5, op0=ALU.mult, op1=ALU.add),
                                   reads=[('ps', Qb)], writes=['rstdb'])
                              P.op('act', lambda e: e.activation(out=rstdb[:], in_=rstdb[:], func=AF.Ln), reads=['rstdb'], writes=['rstdb'])
                              P.op('act', lambda e: e.activation(out=rstdb[:], in_=rstdb[:], func=AF.Exp, scale=-0.5), reads=['rstdb'], writes=['rstdb'])
                              P.op('dve', lambda e, ot=ot, q0=q0: e.scalar_tensor_tensor(out=ot[:, q0:q0 + 512], in0=comb[:], scalar=small[:, 3:4], in1=rstdb[:],
                                                                                         op0=ALU.mult, op1=ALU.mult),
                                   reads=['comb', 'rstdb', 'gsc'], writes=[otk])
                      else:
                          for n, kt in enumerate(range(nkt - 1, -1, -1)):
                              c0 = max(0, kt * 128 - q0)
                              diag = kt * 128 >= q0
                              zb = S0 + (n % 2)
                              ei = n % 2
                              P.group('pe', [lambda e, kt=kt, c0=c0, zb=zb, q0=q0: e.matmul(psb[zb][:, c0:512], KT[:, kt * 128:(kt + 1) * 128],
                                                                                           QT[:, q0 + c0:q0 + 512], start=True, stop=True)],
                                      reads=['KT', 'QT'], writes=[('ps', zb)])
                              P.op('act', lambda e, ei=ei, zb=zb, c0=c0: e.activation(out=ef[ei][:, c0:512], in_=psb[zb][:, c0:512], func=AF.Exp),
                                   reads=[('ps', zb)], writes=[('ef', ei)])
                              P.op('act', lambda e, ei=ei, c0=c0: e.activation(out=spb[ei][:, c0:512], in_=ef[ei][:, c0:512], func=AF.Ln, bias=1.0, scale=1.0),
                                   reads=[('ef', ei)], writes=[('spb', ei)])
                              if diag:
                                  P.op('dve', lambda e, ei=ei, c0=c0: e.tensor_tensor(out=spb[ei][:, c0:c0 + 128], in0=spb[ei][:, c0:c0 + 128],
                                                                                     in1=cm(M_STR), op=ALU.mult),
                                       reads=[('spb', ei), 'cb'], writes=[('spb', ei)])
                              fns = [lambda e, ei=ei, c0=c0, n=n: e.matmul(psb[Xb][:, c0:512], cm(M_NTI), spb[ei][:, c0:512], start=(n == 0), stop=False,
                                                                          skip_group_check=True),
                                     lambda e, kt=kt, c0=c0, q0=q0: e.matmul(psb[Xb][:, c0:512], KT[:, kt * 128:(kt + 1) * 128], QT[:, q0 + c0:q0 + 512],
                                                                             start=False, stop=True, skip_group_check=True)]
                              P.group('pe', fns, reads=[('spb', ei), 'KT', 'QT', 'cb'], writes=[('ps', Xb)])
                              pi = n % 3
                              P.op('act', lambda e, pi=pi, c0=c0: e.activation(out=Pt[pi][:, c0:512], in_=psb[Xb][:, c0:512], func=AF.Exp),
                                   reads=[('ps', Xb)], writes=[('P', pi)])
                              if diag:
                                  P.op('dve', lambda e, pi=pi, c0=c0: e.tensor_tensor(out=Pt[pi][:, c0:c0 + 128], in0=Pt[pi][:, c0:c0 + 128],
                                                                                     in1=cm(M_STR), op=ALU.mult),
                                       reads=[('P', pi), 'cb'], writes=[('P', pi)])
                              fns = []
                              if kt > 0:
                                  fns += [lambda e, kt=kt, c0=c0, q0=q0: e.matmul(psb[Xb][:, c0:512], KTn[:, kt * 128:(kt + 1) * 128], QT[:, q0 + c0:q0 + 512],
                                                                                  start=False, stop=False, skip_group_check=True),
                                          lambda e, ei=ei, c0=c0: e.matmul(psb[Xb][:, c0:512], cm(M_NTE), spb[ei][:, c0:512], start=False, stop=True,
                                                                           skip_group_check=True)]
                              fns.append(lambda e, kt=kt, c0=c0, pi=pi, n=n, nkt=nkt: e.matmul(psb[Ob][:, c0:512], Vh[:, kt * 128:(kt + 1) * 128], Pt[pi][:, c0:512],
                                                                                              start=(n == 0), stop=(n == nkt - 1), skip_group_check=True))
                              P.group('pe', fns, reads=[('P', pi), ('spb', ei), 'KTn', 'QT', 'Vh', 'cb'], writes=[('ps', Xb), ('ps', Ob)])
                          P.op('act', lambda e, ot=ot, q0=q0: e.copy(out=ot[:, q0:q0 + 512], in_=psb[Ob][:, :]), reads=[('ps', Ob)], writes=[otk])
                  P.op('pool', lambda e, ot=ot, hd=hd: e.dma_start(out=mixT[hd, :, :], in_=ot[:]), reads=[otk], writes=[('mix', hd)])
          P.barrier()
          if stop == 'p3':
              break

          for half in range(NHALF):
              tok0 = half * TBA
              for hd in range(NH):
                  P.op('pool', lambda e, hd=hd, tok0=tok0: e.dma_start(out=hT[:, hd, :], in_=mixT[hd, :, tok0:tok0 + TBA]),
                       reads=[('mix', hd)], writes=[('hT', tl) for tl in range(NTA)])
              for c in range(D // 512):
                  w, wk = wload([(lambda w: w[:, :, :], wsrc(w_out, l, c * 512, 512))])
                  for tl in range(NTA):
                      r0 = tok0 + tl * 128
                      pt, pk = nextps()
                      xi = (c * NTA + tl) % 4
                      P.op('pool', lambda e, xi=xi, r0=r0, c=c, xsrc=xsrc: e.dma_start(out=xa[xi][:], in_=xsrc[r0:r0 + 128, c * 512:(c + 1) * 512]),
                           reads=[(xkey, r0 // 128, c)], writes=[('xa', xi)])
                      fns = [lambda e, kc=kc, tl=tl, pt=pt, w=w: e.matmul(pt[:, :], hT[:, kc, tl * 128:(tl + 1) * 128], w[:, kc, :],
                                                                          start=(kc == 0), stop=(kc == KC - 1)) for kc in range(KC)]
                      P.group('pe', fns, reads=[wk, ('hT', tl)], writes=[pk])
                      P.op('dve', lambda e, xi=xi, pt=pt: e.tensor_tensor(out=xa[xi][:], in0=pt[:, :], in1=xa[xi][:], op=ALU.add),
                           reads=[pk, ('xa', xi)], writes=[('xa', xi)])
                      P.op('pool', lambda e, xi=xi, r0=r0, c=c: e.dma_start(out=xres[r0:r0 + 128, c * 512:(c + 1) * 512], in_=xa[xi][:]),
                           reads=[('xa', xi)], writes=[('xres', r0 // 128, c)])
          P.barrier()
          if stop == 'p4':
              break

          gk = load_gain(2 * l + 1)
          for tb in range(S // TB5):
              tok0 = tb * TB5
              norm_T(xres, tok0, NT5, gk, 'xres')
              hkeys = [('hT', tl) for tl in range(NT5)]
              for (fa0, fa1) in passes:
                  for f0 in range(fa0, fa1, FG):
                      wg, wgk = wload([(lambda w: w[:, :, 0:FG * 128], wsrc(w_gate, l, f0 * 128, FG * 128))])
                      wu, wuk = wload([(lambda w: w[:, :, 0:FG * 128], wsrc(w_up, l, f0 * 128, FG * 128))])
                      for fc in range(FG):
                          fl = f0 + fc - fa0
                          for sub in range(NSUB):
                              pg, pgk = nextps(2)
                              fns = [lambda e, kc=kc, fc=fc, pg=pg, wg=wg, sub=sub: e.matmul(
                                  pg[:, :], wg[:, kc, fc * 128:(fc + 1) * 128], hT[:, kc, sub * 512:(sub + 1) * 512],
                                  start=(kc == 0), stop=(kc == KC - 1)) for kc in range(KC)]
                              P.group('pe', fns, reads=[wgk] + hkeys[sub * 4:(sub + 1) * 4], writes=[pgk])
                              pu, puk = nextps(2)
                              fns = [lambda e, kc=kc, fc=fc, pu=pu, wu=wu, sub=sub: e.matmul(
                                  pu[:, :], wu[:, kc, fc * 128:(fc + 1) * 128], hT[:, kc, sub * 512:(sub + 1) * 512],
                                  start=(kc == 0), stop=(kc == KC - 1)) for kc in range(KC)]
                              P.group('pe', fns, reads=[wuk] + hkeys[sub * 4:(sub + 1) * 4], writes=[puk])
                              si = (fl * NSUB + sub) % 2
                              P.op('act', lambda e, si=si, pg=pg: e.activation(out=sg[si][:], in_=pg[:, :], func=AF.Silu),
                                   reads=[pgk], writes=[('sg', si)])
                              P.op('dve', lambda e, si=si, pu=pu, fl=fl, sub=sub: e.tensor_tensor(
                                  out=actT_ap(fl, sub * 512, (sub + 1) * 512), in0=sg[si][:], in1=pu[:, :], op=ALU.mult),
                                  reads=[('sg', si), puk], writes=[('actT', fl, sub)])
                  for c in range(D // 512):
                      for tg in range(NT5 // 4):
                          for f0 in range(fa0, fa1, FG):
                              wd, wdk = wload([(lambda w: w[:, 0:FG, :],
                                                w_down[l, f0 * 128:(f0 + FG) * 128, c * 512:(c + 1) * 512].rearrange("(f p) c -> p f c", p=128))])
                              fns = []
                              for f in range(FG):
                                  fl = f0 + f - fa0
                                  for t in range(4):
                                      tt = tg * 4 + t
                                      fns.append(lambda e, f=f, fl=fl, t=t, tt=tt, wd=wd, st=(f0 == fa0 and f == 0), sp_=(f0 + f == fa1 - 1): e.matmul(
                                          psb[2 + t][:, :], actT_ap(fl, tt * 128, (tt + 1) * 128), wd[:, f, :], start=st, stop=sp_, skip_group_check=True))
                              P.group('pe', fns, reads=[wdk] + [('actT', f0 + f - fa0, s_) for f in range(FG) for s_ in range(NSUB)],
                                      writes=[('ps', 2 + t) for t in range(4)])
                          for t in range(4):
                              r0 = tok0 + (tg * 4 + t) * 128
                              xi = t
                              P.op('pool', lambda e, xi=xi, r0=r0, c=c: e.dma_start(out=xa2[xi][:], in_=xres[r0:r0 + 128, c * 512:(c + 1) * 512]),
                                   reads=[('xres', r0 // 128, c)], writes=[('xa2', xi)])
                              P.op('dve', lambda e, xi=xi, t=t: e.tensor_tensor(out=xa2[xi][:], in0=psb[2 + t][:, :], in1=xa2[xi][:], op=ALU.add),
                                   reads=[('ps', 2 + t), ('xa2', xi)], writes=[('xa2', xi)])
                              P.op('pool', lambda e, xi=xi, r0=r0, c=c: e.dma_start(out=xres[r0:r0 + 128, c * 512:(c + 1) * 512], in_=xa2[xi][:]),
                                   reads=[('xa2', xi)], writes=[('xres', r0 // 128, c)])
          P.barrier()
      except _Stop:
        P.barrier()
        break

    gk = load_gain(2 * L)
    for t in range(NT if stop is None else 0):
        r0 = t * 128
        b = t % 2
        P.op('pool', lambda e, b=b, r0=r0: e.dma_start(out=xt[b][:], in_=xres[r0:r0 + 128, :]),
             reads=[('xres', t, c) for c in range(D // 512)], writes=[('xt', b)])
        P.op('act', lambda e, b=b, t=t: e.activation(out=hb[b][:], in_=xt[b][:], func=AF.Square, accum_out=ssq[:, t:t + 1]),
             reads=[('xt', b)], writes=[('hb', b), ('ssq', t)])
        P.op('dve', lambda e, t=t: e.tensor_scalar(out=ssq[:, t:t + 1], in0=ssq[:, t:t + 1], scalar1=1.0 / D, scalar2=1e-6, op0=ALU.mult, op1=ALU.add),
             reads=[('ssq', t)], writes=[('ssq', t)])
        P.op('act', lambda e, t=t: e.activation(out=ssq[:, t:t + 1], in_=ssq[:, t:t + 1], func=AF.Ln),
             reads=[('ssq', t)], writes=[('ssq', t)])
        P.op('act', lambda e, t=t: e.activation(out=ssq[:, t:t + 1], in_=ssq[:, t:t + 1], func=AF.Exp, scale=-0.5),
             reads=[('ssq', t)], writes=[('ssq', t)])
        P.op('dve', lambda e, b=b, t=t: e.scalar_tensor_tensor(out=xt[b][:], in0=xt[b][:], scalar=ssq[:, t:t + 1], in1=gbuf[:], op0=ALU.mult, op1=ALU.mult),
             reads=[('xt', b), ('ssq', t), gk], writes=[('xt', b)])
        P.op('pool', lambda e, b=b, r0=r0: e.dma_start(out=y[r0:r0 + 128, :], in_=xt[b][:]), reads=[('xt', b)], writes=[('y', t)])
    P.barrier()

    with nc.Block() as block:
        @block.tensor
        def _(e):
            P.replay('pe', e)

        @block.scalar
        def _(e):
            P.replay('act', e)

        @block.vector
        def _(e):
            P.replay('dve', e)

        @block.gpsimd
        def _(e):
            P.replay('sp', e)

        @block.sync
        def _(e):
            P.replay('pool', e)
    es.close()
    return nc


def host_inputs(cfg, x, w_in, b_fgate, w_out, diff_lq1, diff_lk1, diff_lq2, diff_lk2, diff_subln,
                attn_norm, w_gate, w_up, w_down, ffn_norm, final_norm):
    L, HG, D = cfg.L, cfg.HG, cfg.D
    f32 = np.float32
    w_in = np.ascontiguousarray(np.asarray(w_in, f32))
    fgc = cfg.off['fg']
    w_fg = np.zeros((L, D, 128), f32)
    b_rep = np.zeros((L, 128, 1), f32)
    for h in range(HG):
        w_fg[:, :, h * 32:(h + 1) * 32] = w_in[:, :, fgc + h:fgc + h + 1]
        b_rep[:, h * 32:(h + 1) * 32, 0] = np.asarray(b_fgate, f32)[:, h][:, None]
    lam = np.stack([np.asarray(a, f32) for a in (diff_lq1, diff_lk1, diff_lq2, diff_lk2)], 1)
    lam_in = np.ascontiguousarray(np.broadcast_to(lam.reshape(1, L * 256), (128, L * 256)))
    gsub = np.ascontiguousarray(np.asarray(diff_subln, f32)[:, :, None])
    gl = []
    for l in range(L):
        gl += [np.asarray(attn_norm, f32)[l], np.asarray(ffn_norm, f32)[l]]
    gl.append(np.asarray(final_norm, f32))
    gains = np.ascontiguousarray(np.broadcast_to(np.stack(gl, 0)[:, None, :], (2 * L + 1, 128, D)))
    rope, cbm, cfm, onesf = host_consts(cfg)
    common = dict(w_in=w_in, w_fg=w_fg, b_rep=b_rep, w_out=np.ascontiguousarray(np.asarray(w_out, f32)), lam_in=lam_in, gsub=gsub,
                  gains=gains, w_gate=np.ascontiguousarray(np.asarray(w_gate, f32)), w_up=np.ascontiguousarray(np.asarray(w_up, f32)),
                  w_down=np.ascontiguousarray(np.asarray(w_down, f32)), rope=rope, cb=cbm, cf=cfm, onesf=onesf)
    x = np.asarray(x, f32)
    return [dict(common, x=np.ascontiguousarray(x[b])) for b in range(x.shape[0])]


def kernel(**inputs):
    cfg = Cfg()
    in_maps = host_inputs(cfg, **inputs)
    nc = build(cfg)
    res = run_bass_kernel_spmd(nc, in_maps, core_ids=list(range(len(in_maps))))
    return np.stack([np.asarray(r["y"], np.float32) for r in res.results], 0)
```

```python
import math
import numpy as np
import ml_dtypes
import concourse.bass as bass
import concourse.mybir as mybir
from concourse.bass_utils import run_bass_kernel_spmd

F32 = mybir.dt.float32
BF16 = mybir.dt.bfloat16
AF = mybir.ActivationFunctionType
ALU = mybir.AluOpType
AX = mybir.AxisListType

NEG = -30000.0


class Cfg:
    def __init__(self, D=2048, S=2048, HG=4, FF=5632, L=4, NB=4):
        self.D, self.S, self.HG, self.FF, self.L, self.NB = D, S, HG, FF, L, NB
        self.KC = D // 128
        self.NT = S // 128
        self.NQB = S // 512
        self.GW = HG * 128
        self.FC = FF // 128
        self.INC = 12 * self.GW + HG
        self.TBA = min(1024, S)
        self.NHALF = S // self.TBA
        self.PF = 11 if self.FC % 11 == 0 else (4 if self.FC % 4 == 0 else 1)
        self.PF = min(self.PF, self.KC)
        while self.FC % self.PF:
            self.PF -= 1
        self.FG = 4 if self.FC % 4 == 0 else 1
        GW = self.GW
        o = {}
        o['fq'], o['fk'], o['fv'], o['fg'] = 0, GW, 2 * GW, 3 * GW
        b = 3 * GW + HG
        for i, n in enumerate(['mq', 'mk', 'mv', 'dq', 'dk', 'dv', 'sq', 'sk', 'sv']):
            o[n] = b + i * GW
        self.off = o


def host_consts(cfg):
    S = cfg.S
    theta = 10000.0
    t = np.arange(S, dtype=np.float32)[None, :]
    inv = (1.0 / (theta ** (np.arange(0, 128, 2, dtype=np.float32) / 128))).astype(np.float32)
    i = np.arange(128)
    ang = (inv[i % 64][:, None] * t).astype(np.float32)
    cm = np.cos(ang).astype(np.float32)
    sm = (np.sin(ang) * np.where(i < 64, -1.0, 1.0)[:, None]).astype(np.float32)
    invd = (1.0 / (theta ** (np.arange(0, 64, 2, dtype=np.float32) / 64))).astype(np.float32)
    ii = i % 64
    angd = (invd[ii % 32][:, None] * t).astype(np.float32)
    cd = np.cos(angd).astype(np.float32)
    sd = (np.sin(angd) * np.where(ii < 32, -1.0, 1.0)[:, None]).astype(np.float32)
    rope = np.stack([cm, sm, cd, sd], 0).astype(np.float32)
    mats = []
    ident = np.eye(128, dtype=np.float32)
    swm = np.zeros((128, 128), np.float32)
    swm[(i + 64) % 128, i] = 1.0
    swd = np.zeros((128, 128), np.float32)
    swd[(i // 64) * 64 + (ii + 32) % 64, i] = 1.0
    p = np.arange(128)[:, None]
    j = np.arange(128)[None, :]
    mincl = (p <= j).astype(np.float32)
    mstr = (p < j).astype(np.float32)
    ones = np.ones((128, 128), np.float32)
    ntri_incl = -(p >= j).astype(np.float32)
    ntri_excl = -(p < j).astype(np.float32)
    mats = [ident, swm, swd, mincl, mstr, ones, ntri_incl, ntri_excl]
    for kb in range(8):
        m = np.zeros((128, 128), np.float32)
        m[kb, :] = 1.0
        mats.append(m)
    mats.append(NEG * (p > j).astype(np.float32))
    cb = np.concatenate(mats, 1).astype(ml_dtypes.bfloat16)
    cf = np.zeros((128, 16), np.float32)
    r = np.arange(128) % 32
    for jj in range(6):
        cf[:, jj] = (r == jj)
    cf[:, 6] = ((r >= 3) & (r < 6))
    cf[:, 7] = (r < 3)
    cf[:, 8] = 1.0
    onesf = np.ones((128, 128), np.float32)
    return rope, cb, cf, onesf


M_ID, M_SWM, M_SWD, M_INCL, M_STR, M_ONES, M_NTI, M_NTE, M_IND, M_NEGU = 0, 1, 2, 3, 4, 5, 6, 7, 8, 16
NMAT = 17


class _Stop(Exception):
    pass


class Prog:
    ENG = ['pe', 'act', 'dve', 'pool', 'sp']
    DMAQ = ['pool', 'sp']
    NDS = 12

    def __init__(self, nc, sems):
        self.nc = nc
        self.q = {e: [] for e in self.ENG}
        self.sem = sems
        self.cnt = {k: 0 for k in sems}
        self.waited = {e: {} for e in self.ENG}
        self.lastw = {}
        self.readers = {}
        self.dma_i = {e: 0 for e in self.DMAQ}
        self.last_tok = {}

    def _wait(self, eng, toks):
        best = {}
        for (s, v) in toks:
            if v > best.get(s, 0):
                best[s] = v
        for s, v in best.items():
            if self.waited[eng].get(s, 0) >= v:
                continue
            self.waited[eng][s] = v
            self.q[eng].append(('w', s, v))

    def _deps(self, reads, writes):
        toks = []
        for k in reads:
            toks += self.lastw.get(k, [])
        for k in writes:
            toks += self.readers.get(k, [])
            toks += self.lastw.get(k, [])
        return toks

    def _commit(self, tok, reads, writes, wappend=False):
        for k in reads:
            self.readers.setdefault(k, []).append(tok)
        for k in writes:
            if wappend:
                self.lastw.setdefault(k, []).append(tok)
            else:
                self.lastw[k] = [tok]
            self.readers[k] = []

    def _newtok(self, eng):
        if eng in self.DMAQ:
            i = self.dma_i[eng]
            self.dma_i[eng] += 1
            s = '%s%d' % (eng, i % self.NDS)
            if self.cnt[s] > 0:
                self._wait(eng, [(s, self.cnt[s])])
            self.cnt[s] += 16
            return (s, self.cnt[s]), 16
        self.cnt[eng] += 1
        return (eng, self.cnt[eng]), 1

    def op(self, eng, fn, reads=(), writes=(), wappend=False):
        self._wait(eng, self._deps(reads, () if wappend else writes))
        tok, inc = self._newtok(eng)
        self.q[eng].append(('o', fn, tok[0], inc))
        self._commit(tok, reads, writes, wappend)
        self.last_tok[tok[0]] = tok
        return tok

    def group(self, eng, fns, reads=(), writes=()):
        self._wait(eng, self._deps(reads, writes))
        for fn in fns[:-1]:
            self.q[eng].append(('o', fn, None, 0))
        tok, inc = self._newtok(eng)
        self.q[eng].append(('o', fns[-1], tok[0], inc))
        self._commit(tok, reads, writes)
        self.last_tok[tok[0]] = tok
        return tok

    def nop(self, *a, **k):
        return None

    def barrier(self):
        toks = list(self.last_tok.values())
        for e in self.ENG:
            self._wait(e, toks)

    def replay(self, eng, e):
        for it in self.q[eng]:
            if it[0] == 'w':
                e.wait_ge(self.sem[it[1]], it[2])
            else:
                ins = it[1](e)
                if it[2] is not None:
                    ins.then_inc(self.sem[it[2]], it[3])


def build(cfg):
    D, S, HG, FF, L = cfg.D, cfg.S, cfg.HG, cfg.FF, cfg.L
    KC, NT, NQB, GW, FC, TBA, NHALF, PF, FG = cfg.KC, cfg.NT, cfg.NQB, cfg.GW, cfg.FC, cfg.TBA, cfg.NHALF, cfg.PF, cfg.FG
    NTA = TBA // 128
    TB5 = TBA
    NT5 = TB5 // 128
    NSUB = TB5 // 512
    ACAP = -(-((FC + 1) // 2) // FG) * FG
    passes = [(0, min(ACAP, FC))] + ([(ACAP, FC)] if FC > ACAP else [])
    NBA = TBA // 512
    NH = 4 * HG
    nc = bass.Bass("TRN2", target_bir_lowering=False)

    def din(name, shape, dt=F32):
        return nc.dram_tensor(name, list(shape), dt, kind="ExternalInput").ap()

    x_in = din("x", [S, D])
    w_in = din("w_in", [L, D, cfg.INC])
    w_fg = din("w_fg", [L, D, 128])
    b_rep = din("b_rep", [L, 128, 1])
    w_out = din("w_out", [L, D, D])
    lam_in = din("lam_in", [128, L * 4 * 64])
    gsub = din("gsub", [L, 128, 1])
    gains = din("gains", [2 * L + 1, 128, D])
    w_gate = din("w_gate", [L, D, FF])
    w_up = din("w_up", [L, D, FF])
    w_down = din("w_down", [L, FF, D])
    rope = din("rope", [4, 128, S])
    cb_in = din("cb", [128, NMAT * 128], BF16)
    cf_in = din("cf", [128, 16])
    onesf_in = din("onesf", [128, 128])
    y = nc.dram_tensor("y", [S, D], F32, kind="ExternalOutput").ap()

    xres = nc.dram_tensor("xres", [S, D], F32).ap()
    QTs = nc.dram_tensor("QTs", [NH, 128, S], BF16).ap()
    KTs = nc.dram_tensor("KTs", [NH, 128, S], BF16).ap()
    Vs = nc.dram_tensor("Vs", [4, S, GW], BF16).ap()
    FAs = nc.dram_tensor("FAs", [HG, 6, S], BF16).ap()
    FBs = nc.dram_tensor("FBs", [HG, 6, S], BF16).ap()
    mixT = nc.dram_tensor("mixT", [NH, 128, S], BF16).ap()

    from contextlib import ExitStack
    es = ExitStack()

    def sb(name, shape, dt):
        return es.enter_context(nc.sbuf_tensor(name, list(shape), dt))

    def ps(name, shape, dt):
        return es.enter_context(nc.psum_tensor(name, list(shape), dt))

    NW = 2
    hT = sb("hT", [128, KC, TBA], BF16)
    wsl = [sb("w%d" % i, [128, KC, 512], BF16) for i in range(NW)]
    xt = [sb("xt%d" % i, [128, D], F32) for i in range(2)]
    hb = [sb("hb%d" % i, [128, D], BF16) for i in range(2)]
    gbuf = sb("gbuf", [128, D], F32)
    ropeC = [sb("ropeC%d" % i, [128, 512], F32) for i in range(1)]
    ropeS = [sb("ropeS%d" % i, [128, 512], F32) for i in range(1)]
    cb = sb("cbm", [128, NMAT * 128], BF16)
    cf = sb("cfm", [128, 16], F32)
    onesf = sb("onesf_sb", [128, 512], F32)
    small = sb("small", [128, 64], F32)
    lamb = sb("lamb", [128, 4 * 64], F32)
    ssq = sb("ssq", [128, 32], F32)
    UNI = max(ACAP * TB5 + 2 * 1024 + 4 * 1024, 26 * 1024)
    uni = sb("uni", [128, UNI], BF16)
    wfgb = sb("wfgb", [128, KC, 128], BF16)
    NST = 3
    wst32 = [sb("wst%d" % i, [128, 4, 512], F32) for i in range(NST)]

    class Carver:
        def __init__(self):
            self.o = 0

        def get(self, n, dt):
            k = 2 if dt == F32 else 1
            a = uni[:, self.o:self.o + n * k]
            self.o += n * k
            assert self.o <= UNI, self.o
            return a.bitcast(F32) if dt == F32 else a

    ca = Carver()
    QT = ca.get(S, BF16)
    KT = ca.get(S, BF16)
    Vh = ca.get(NT * 128, BF16)
    X1 = ca.get(S, BF16)
    X2 = ca.get(S, BF16)
    fa = X1
    negmT = X1
    KTn = X1
    fb = X2
    OT = [ca.get(S, BF16) for _ in range(2)]
    Pt = [ca.get(512, BF16) for _ in range(3)]
    spb = [ca.get(512, BF16) for _ in range(2)]
    ef = [ca.get(512, F32) for _ in range(2)]
    rden = ca.get(512, F32)
    o1 = ca.get(512, F32)
    o2 = ca.get(512, F32)
    comb = ca.get(512, F32)
    sq = ca.get(512, F32)
    rstdb = ca.get(512, F32)
    kmf = ca.get(8, F32)
    kmb = ca.get(8, BF16)
    gm = ca.get(8, F32)
    top8 = ca.get(8, F32)
    negm = ca.get(8, F32)
    negmb = ca.get(8, BF16)
    ca = Carver()
    stg = [ca.get(512, BF16) for _ in range(2)]
    qraw = [ca.get(512, BF16) for _ in range(2)]
    q32 = [ca.get(512, F32) for _ in range(2)]
    t1 = [ca.get(512, F32) for _ in range(2)]
    t2 = [ca.get(512, F32) for _ in range(2)]
    fgt = {n: ca.get(512, F32) for n in ['e', 'sp', 'hi32', 'r1', 'mid32', 'r2', 'acc', 'acc2']}
    csp = [ca.get(512, F32) for _ in range(2)]
    fgb = {n: ca.get(512, BF16) for n in ['hi', 'mid', 'fa', 'fb']}
    ca = Carver()
    xa = [ca.get(512, F32) for _ in range(4)]
    cf_ = Carver()
    actT = cf_.get(ACAP * TB5, BF16)
    sg = [cf_.get(512, F32) for _ in range(2)]
    xa2 = [cf_.get(512, F32) for _ in range(4)]

    def actT_ap(f, a, b):
        return actT[:, f * TB5 + a: f * TB5 + b]

    NPS = 6
    psb = [ps("ps%d" % i, [128, 512], F32) for i in range(NPS)]
    tpb = [ps("tpb%d" % i, [128, 1024], BF16) for i in range(2)]
    tp = [tpb[0][:, 0:512], tpb[1][:, 0:512]]

    semnames = ['pe', 'act', 'dve'] + ['pool%d' % i for i in range(Prog.NDS)] + ['sp%d' % i for i in range(Prog.NDS)]
    sems = {n: es.enter_context(nc.semaphore(n)) for n in semnames}
    P = Prog(nc, sems)

    def cm(i):
        return cb[:, i * 128:(i + 1) * 128]

    def col(t, i):
        return t[:, i:i + 1]

    P.op('pool', lambda e: e.dma_start(out=cb[:], in_=cb_in[:, :]), writes=['cb'])
    P.op('pool', lambda e: e.dma_start(out=cf[:], in_=cf_in[:, :]), writes=['cf'])
    for i in range(4):
        P.op('pool', lambda e, i=i: e.dma_start(out=onesf[:, i * 128:(i + 1) * 128], in_=onesf_in[:, :]), writes=['onesf'])

    wstate = {'i': 0}

    def wload(parts):
        i = wstate['i']
        wstate['i'] += 1
        slot = i % NW
        w = wsl[slot]
        first = True
        for (dstf, src) in parts:
            d = dstf(w)
            n1 = d.shape[1]
            nco = d.shape[2]
            for a in range(0, n1, 4):
                b = min(n1, a + 4)
                st = wstate.get('st', 0)
                wstate['st'] = st + 1
                sg32 = wst32[st % NST]
                P.op('pool', lambda e, o=sg32[:, 0:b - a, 0:nco], s=src[:, a:b, :]: e.dma_start(out=o, in_=s), writes=[('wst', st % NST)])
                P.op('act', lambda e, o=d[:, a:b, :], i_=sg32[:, 0:b - a, 0:nco]: e.copy(out=o, in_=i_),
                     reads=[('wst', st % NST)], writes=[('w', slot)], wappend=not first)
                first = False
        return w, ('w', slot)

    ring = {'i': 0}

    def nextps(n=6, base=0):
        i = ring['i']
        ring['i'] += 1
        k = base + i % n
        return psb[k], ('ps', k)

    def wsrc(wap, l, c0, ncols):
        return wap[l, :, c0:c0 + ncols].rearrange("(kc p) c -> p kc c", p=128)

    def norm_T(xsrc, tok0, ntile, gkey, xkey):
        for tl in range(ntile):
            r0 = tok0 + tl * 128
            b = tl % 2
            P.op('pool', lambda e, b=b, r0=r0: e.dma_start(out=xt[b][:], in_=xsrc[r0:r0 + 128, :]),
                 reads=[(xkey, r0 // 128, c) for c in range(D // 512)], writes=[('xt', b)])
            P.op('act', lambda e, b=b, tl=tl: e.activation(out=hb[1 - b][:], in_=xt[b][:], func=AF.Square,
                                                          accum_out=ssq[:, tl:tl + 1]),
                 reads=[('xt', b)], writes=[('hb', 1 - b), ('ssq', tl)])
            P.op('dve', lambda e, tl=tl: e.tensor_scalar(out=ssq[:, tl:tl + 1], in0=ssq[:, tl:tl + 1], scalar1=1.0 / D,
                                                         scalar2=1e-6, op0=ALU.mult, op1=ALU.add),
                 reads=[('ssq', tl)], writes=[('ssq', tl)])
            P.op('act', lambda e, tl=tl: e.activation(out=ssq[:, tl:tl + 1], in_=ssq[:, tl:tl + 1], func=AF.Ln),
                 reads=[('ssq', tl)], writes=[('ssq', tl)])
            P.op('act', lambda e, tl=tl: e.activation(out=ssq[:, tl:tl + 1], in_=ssq[:, tl:tl + 1], func=AF.Exp, scale=-0.5),
                 reads=[('ssq', tl)], writes=[('ssq', tl)])
            P.op('dve', lambda e, b=b, tl=tl: e.scalar_tensor_tensor(out=hb[b][:], in0=xt[b][:], scalar=ssq[:, tl:tl + 1],
                                                                     in1=gbuf[:], op0=ALU.mult, op1=ALU.mult),
                 reads=[('xt', b), ('ssq', tl), gkey], writes=[('hb', b)])
            for c4 in range(KC // 4 if KC >= 4 else 1):
                nch = min(4, KC)
                tpi = (tl * 4 + c4) % 2
                fns = [lambda e, j=j, c4=c4, tpi=tpi, b=b: e.transpose(out=tp[tpi][:, j * 128:(j + 1) * 128],
                                                                     in_=hb[b][:, (c4 * 4 + j) * 128:(c4 * 4 + j + 1) * 128],
                                                                     identity=cm(M_ID)) for j in range(nch)]
                P.group('pe', fns, reads=[('hb', b), 'cb'], writes=[('tp', tpi)])
                eng = 'act' if c4 % 2 == 0 else 'dve'
                if eng == 'act':
                    f = lambda e, c4=c4, tl=tl, tpi=tpi, nch=nch: e.copy(
                        out=hT[:, c4 * 4:c4 * 4 + nch, tl * 128:(tl + 1) * 128],
                        in_=tp[tpi][:, 0:nch * 128].rearrange("p (c t) -> p c t", t=128))
                else:
                    f = lambda e, c4=c4, tl=tl, tpi=tpi, nch=nch: e.tensor_copy(
                        out=hT[:, c4 * 4:c4 * 4 + nch, tl * 128:(tl + 1) * 128],
                        in_=tp[tpi][:, 0:nch * 128].rearrange("p (c t) -> p c t", t=128))
                P.op(eng, f, reads=[('tp', tpi)], writes=[('hT', tl)])

    def load_gain(idx):
        P.op('pool', lambda e: e.dma_start(out=gbuf[:], in_=gains[idx, :, :]), writes=['g'])
        return 'g'

    groups = ['f', 'm', 'd', 's']
    stop = getattr(cfg, 'stop', None)
    dbg = getattr(cfg, 'dbg', '')
    for l in range(L):
      try:
          if stop == 'p0':
              break
          xsrc = x_in if l == 0 else xres
          xkey = 'xin' if l == 0 else 'xres'
          lam_init = 0.8 - 0.6 * math.exp(-0.3 * l)
          P.op('pool', lambda e, l=l: e.dma_start(out=small[:, 0:1], in_=b_rep[l, :, :]), writes=['small0'])
          P.op('dve', lambda e: e.tensor_scalar(out=small[:, 1:2], in0=small[:, 0:1], scalar1=-1.0, scalar2=None, op0=ALU.mult),
               reads=['small0'], writes=['negb'])
          P.op('pool', lambda e, l=l: e.dma_start(out=small[:, 2:3], in_=gsub[l, :, :]), writes=['small2'])
          P.op('dve', lambda e, li=lam_init: e.tensor_scalar(out=small[:, 3:4], in0=small[:, 2:3], scalar1=1.0 - li, scalar2=None,
                                                             op0=ALU.mult), reads=['small2'], writes=['gsc'])
          P.op('pool', lambda e, l=l: e.dma_start(out=lamb[:], in_=lam_in[:, l * 256:(l + 1) * 256]), writes=['lamb'])
          P.op('dve', lambda e: e.tensor_tensor(out=lamb[:, 0:64], in0=lamb[:, 0:64], in1=lamb[:, 64:128], op=ALU.mult),
               reads=['lamb'], writes=['lamb'])
          P.op('dve', lambda e: e.tensor_tensor(out=lamb[:, 128:192], in0=lamb[:, 128:192], in1=lamb[:, 192:256], op=ALU.mult),
               reads=['lamb'], writes=['lamb'])
          P.op('dve', lambda e: e.reduce_sum(out=small[:, 4:5], in_=lamb[:, 0:64], axis=AX.X), reads=['lamb'], writes=['s4'])
          P.op('dve', lambda e: e.reduce_sum(out=small[:, 5:6], in_=lamb[:, 128:192], axis=AX.X), reads=['lamb'], writes=['s5'])
          P.op('act', lambda e: e.activation(out=small[:, 6:8], in_=small[:, 4:6], func=AF.Exp), reads=['s4', 's5'], writes=['s67'])
          P.op('dve', lambda e: e.tensor_tensor(out=small[:, 8:9], in0=small[:, 7:8], in1=small[:, 6:7], op=ALU.subtract),
               reads=['s67'], writes=['s8'])
          P.op('dve', lambda e, li=lam_init: e.tensor_scalar(out=small[:, 8:9], in0=small[:, 8:9], scalar1=-li, scalar2=None, op0=ALU.add),
               reads=['s8'], writes=['neglam'])
          for a in range(0, KC, 4):
              st = wstate.get('st', 0)
              wstate['st'] = st + 1
              sg32 = wst32[st % NST]
              P.op('pool', lambda e, l=l, a=a, sg32=sg32: e.dma_start(out=sg32[:, 0:4, 0:128],
                                                                    in_=w_fg[l, :, :].rearrange("(kc p) c -> p kc c", p=128)[:, a:a + 4, :]),
                   writes=[('wst', st % NST)])
              P.op('act', lambda e, a=a, sg32=sg32: e.copy(out=wfgb[:, a:a + 4, :], in_=sg32[:, 0:4, 0:128]),
                   reads=[('wst', st % NST)], writes=['wfg'], wappend=(a > 0))

          for half in range(NHALF):
              tok0 = half * TBA
              gk = load_gain(2 * l)
              norm_T(xsrc, tok0, NTA, gk, xkey)
              if stop == 'p1':
                  raise _Stop()
              hTkeys = [('hT', tl) for tl in range(NTA)]
              for gi, g in enumerate(groups):
                  for xi, xn in enumerate(['q', 'k']):
                      c0 = cfg.off[g + xn]
                      w, wk = wload([(lambda w: w[:, :, 0:GW] if GW == 512 else w[:, :, 0:GW], wsrc(w_in, l, c0, GW))])
                      isrope = g in ('m', 'd')
                      scale = (128 ** -0.5 if g != 'd' else 64 ** -0.5) if xn == 'q' else 1.0
                      for nb in range(NBA):
                          if isrope:
                              rb = 0
                          if isrope and 'nodma' not in dbg:
                              ti = 0 if g == 'm' else 2
                              g0 = tok0 + nb * 512
                              P.op('pool', lambda e, rb=rb, ti=ti, g0=g0: e.dma_start(out=ropeC[rb][:], in_=rope[ti, :, g0:g0 + 512]),
                                   writes=[('ropeC', rb)])
                              P.op('pool', lambda e, rb=rb, ti=ti, g0=g0: e.dma_start(out=ropeS[rb][:], in_=rope[ti + 1, :, g0:g0 + 512]),
                                   writes=[('ropeS', rb)])
                          for h in range(HG):
                              hd = gi * HG + h
                              pt, pk = nextps()
                              fns = [lambda e, kc=kc, h=h, nb=nb, pt=pt, w=w: e.matmul(
                                  pt[:, :], w[:, kc, h * 128:(h + 1) * 128], hT[:, kc, nb * 512:(nb + 1) * 512],
                                  start=(kc == 0), stop=(kc == KC - 1)) for kc in range(KC)]
                              P.group('pe', fns, reads=[wk] + hTkeys[nb * 4:(nb + 1) * 4], writes=[pk])
                              si = (nb * HG + h) % 2
                              if not isrope:
                                  P.op('act', lambda e, si=si, pt=pt, sc=scale: e.mul(out=stg[si][:], in_=pt[:, :], mul=sc),
                                       reads=[pk], writes=[('stg', si)])
                              else:
                                  P.op('act', lambda e, si=si, pt=pt: e.copy(out=q32[si][:], in_=pt[:, :]), reads=[pk], writes=[('q32', si)])
                                  P.op('act', lambda e, si=si: e.copy(out=qraw[si][:], in_=q32[si][:]), reads=[('q32', si)], writes=[('qraw', si)])
                                  pt2, pk2 = nextps()
                                  swi = M_SWM if g == 'm' else M_SWD
                                  if 'nosw' not in dbg: P.group('pe', [lambda e, pt2=pt2, si=si, swi=swi: e.matmul(pt2[:, :], cm(swi), qraw[si][:], start=True, stop=True)],
                                          reads=[('qraw', si), 'cb'], writes=[pk2])
                                  (P.nop if 'nodve1' in dbg else P.op)('dve', lambda e, si=si, pt=pt, rb=rb, sc=scale: e.scalar_tensor_tensor(
                                      out=t1[si][:], in0=q32[si][:], scalar=sc, in1=ropeC[rb][:], op0=ALU.mult, op1=ALU.mult),
                                      reads=[('q32', si), ('ropeC', rb)], writes=[('t1', si)])
                                  (P.nop if 'nodve2' in dbg else P.op)('dve', lambda e, si=si, pt2=pt2, rb=rb, sc=scale: e.scalar_tensor_tensor(
                                      out=t2[si][:], in0=pt2[:, :], scalar=sc, in1=ropeS[rb][:], op0=ALU.mult, op1=ALU.mult),
                                      reads=[pk2, ('ropeS', rb)], writes=[('t2', si)])
                                  (P.nop if 'nodve3' in dbg else P.op)('dve', lambda e, si=si: e.tensor_tensor(out=stg[si][:], in0=t1[si][:], in1=t2[si][:], op=ALU.add),
                                       reads=[('t1', si), ('t2', si)], writes=[('stg', si)])
                              dst = QTs if xn == 'q' else KTs
                              g0 = tok0 + nb * 512
                              P.op('pool', lambda e, si=si, dst=dst, hd=hd, g0=g0: e.dma_start(out=dst[hd, :, g0:g0 + 512], in_=stg[si][:]),
                                   reads=[('stg', si)], writes=[('qk', xn, hd, g0 // 512)])
                  if stop == 'p2q':
                      raise _Stop()
                  w, wk = wload([(lambda w: w[:, :, 0:GW], wsrc(w_in, l, cfg.off[g + 'v'], GW))])
                  for tl in range(NTA):
                      pt, pk = nextps()
                      fns = [lambda e, kc=kc, tl=tl, pt=pt, w=w: e.matmul(pt[:, 0:GW], hT[:, kc, tl * 128:(tl + 1) * 128], w[:, kc, 0:GW],
                                                                          start=(kc == 0), stop=(kc == KC - 1)) for kc in range(KC)]
                      P.group('pe', fns, reads=[wk, ('hT', tl)], writes=[pk])
                      si = tl % 2
                      P.op('act', lambda e, si=si, pt=pt: e.copy(out=stg[si][:, 0:GW], in_=pt[:, 0:GW]), reads=[pk], writes=[('stg', si)])
                      r0 = tok0 + tl * 128
                      P.op('pool', lambda e, si=si, gi=gi, r0=r0: e.dma_start(out=Vs[gi, r0:r0 + 128, :], in_=stg[si][:, 0:GW]),
                           reads=[('stg', si)], writes=[('v', gi, r0 // 128)])
                  if stop == 'p2v':
                      raise _Stop()
                  if g == 'f':
                      for nb in range(NBA):
                          gb = half * NBA + nb
                          pt, pk = nextps()
                          fns = [lambda e, kc=kc, nb=nb, pt=pt: e.matmul(pt[:, :], wfgb[:, kc, :], hT[:, kc, nb * 512:(nb + 1) * 512],
                                                                         start=(kc == 0), stop=(kc == KC - 1)) for kc in range(KC)]
                          P.group('pe', fns, reads=['wfg'] + hTkeys[nb * 4:(nb + 1) * 4], writes=[pk])
                          P.op('act', lambda e, pt=pt: e.activation(out=fgt['e'][:], in_=pt[:, :], func=AF.Exp, bias=small[:, 1:2], scale=-1.0),
                               reads=[pk, 'negb'], writes=['fe'])
                          P.op('act', lambda e: e.activation(out=fgt['sp'][:], in_=fgt['e'][:], func=AF.Ln, bias=1.0, scale=1.0),
                               reads=['fe'], writes=['fsp'])
                          cpar = gb % 2
                          if gb == 0:
                              P.op('dve', lambda e, cpar=cpar: e.tensor_tensor_scan(out=csp[cpar][:], data0=onesf[:], data1=fgt['sp'][:],
                                                                                  initial=0.0, op0=ALU.mult, op1=ALU.add),
                                   reads=['fsp', 'onesf'], writes=[('csp', cpar)])
                          else:
                              P.op('dve', lambda e, cpar=cpar: e.tensor_tensor_scan(out=csp[cpar][:], data0=onesf[:], data1=fgt['sp'][:],
                                                                                  initial=csp[1 - cpar][:, 511:512], op0=ALU.mult, op1=ALU.add),
                                   reads=['fsp', 'onesf', ('csp', 1 - cpar)], writes=[('csp', cpar)])
                          c = csp[cpar]
                          seq = [
                              (lambda e, c=c: e.tensor_copy(out=fgb['hi'][:], in_=c[:])),
                              (lambda e: e.tensor_copy(out=fgt['hi32'][:], in_=fgb['hi'][:])),
                              (lambda e, c=c: e.tensor_tensor(out=fgt['r1'][:], in0=c[:], in1=fgt['hi32'][:], op=ALU.subtract)),
                              (lambda e: e.tensor_copy(out=fgb['mid'][:], in_=fgt['r1'][:])),
                              (lambda e: e.tensor_copy(out=fgt['mid32'][:], in_=fgb['mid'][:])),
                              (lambda e: e.tensor_tensor(out=fgt['r2'][:], in0=fgt['r1'][:], in1=fgt['mid32'][:], op=ALU.subtract)),
                              (lambda e: e.tensor_scalar(out=fgt['acc'][:], in0=fgt['hi32'][:], scalar1=col(cf, 0), scalar2=None, op0=ALU.mult)),
                              (lambda e: e.scalar_tensor_tensor(out=fgt['acc'][:], in0=fgt['mid32'][:], scalar=col(cf, 1), in1=fgt['acc'][:],
                                                                op0=ALU.mult, op1=ALU.add)),
                              (lambda e: e.scalar_tensor_tensor(out=fgt['acc'][:], in0=fgt['r2'][:], scalar=col(cf, 2), in1=fgt['acc'][:],
                                                                op0=ALU.mult, op1=ALU.add)),
                              (lambda e: e.tensor_scalar(out=fgb['fa'][:], in0=fgt['acc'][:], scalar1=col(cf, 6), scalar2=None, op0=ALU.add)),
                              (lambda e: e.tensor_scalar(out=fgt['acc2'][:], in0=fgt['hi32'][:], scalar1=col(cf, 3), scalar2=None, op0=ALU.mult)),
                              (lambda e: e.scalar_tensor_tensor(out=fgt['acc2'][:], in0=fgt['mid32'][:], scalar=col(cf, 4), in1=fgt['acc2'][:],
                                                                op0=ALU.mult, op1=ALU.add)),
                              (lambda e: e.scalar_tensor_tensor(out=fgt['acc2'][:], in0=fgt['r2'][:], scalar=col(cf, 5), in1=fgt['acc2'][:],
                                                                op0=ALU.mult, op1=ALU.add)),
                              (lambda e: e.tensor_scalar(out=fgb['fb'][:], in0=fgt['acc2'][:], scalar1=-1.0, scalar2=col(cf, 7), op0=ALU.mult,
                                                         op1=ALU.add)),
                          ]
                          for fn in seq:
                              P.op('dve', fn, reads=['fchain', ('csp', cpar), 'cf'], writes=['fchain'])
                          g0 = tok0 + nb * 512
                          for h in range(HG):
                              P.op('pool', lambda e, h=h, g0=g0: e.dma_start(out=FAs[h, :, g0:g0 + 512], in_=fgb['fa'][h * 32:h * 32 + 6, :]),
                                   reads=['fchain'], writes=[('fa', h, g0 // 512)])
                              P.op('pool', lambda e, h=h, g0=g0: e.dma_start(out=FBs[h, :, g0:g0 + 512], in_=fgb['fb'][h * 32:h * 32 + 6, :]),
                                   reads=['fchain'], writes=[('fbk', h, g0 // 512)])
                  if stop == 'p2' + g:
                      raise _Stop()
          P.barrier()
          if stop == 'p2':
              break

          S0, S1, Ob, Db, Qb, Xb = 0, 1, 2, 3, 4, 5
          for gi, g in enumerate(groups):
              for h in range(HG):
                  hd = gi * HG + h
                  P.op('pool', lambda e, hd=hd: e.dma_start(out=QT[:], in_=QTs[hd, :, :]),
                       reads=[('qk', 'q', hd, b) for b in range(NQB)], writes=['QT'])
                  P.op('pool', lambda e, hd=hd: e.dma_start(out=KT[:], in_=KTs[hd, :, :]),
                       reads=[('qk', 'k', hd, b) for b in range(NQB)], writes=['KT'])
                  for a in range(0, NT, 4):
                      P.op('pool', lambda e, gi=gi, h=h, a=a: e.dma_start(
                          out=Vh[:].rearrange("p (t c) -> p t c", c=128)[:, a:a + 4, :],
                          in_=Vs[gi, :, h * 128:(h + 1) * 128].rearrange("(t p) c -> p t c", p=128)[:, a:a + 4, :]),
                          reads=[('v', gi, t) for t in range(NT)], writes=['Vh'], wappend=(a > 0))
                  if g == 'f':
                      P.op('pool', lambda e, h=h: e.dma_start(out=fa[0:6, :], in_=FAs[h, :, :]),
                           reads=[('fa', h, b) for b in range(NQB)], writes=['fa'])
                      P.op('pool', lambda e, h=h: e.dma_start(out=fb[0:6, :], in_=FBs[h, :, :]),
                           reads=[('fbk', h, b) for b in range(NQB)], writes=['fb'])
                  if g == 's':
                      P.op('dve', lambda e: e.tensor_scalar(out=KTn[:], in0=KT[:], scalar1=-1.0, scalar2=None, op0=ALU.mult),
                           reads=['KT'], writes=['KTn'])
                  if g == 'm':
                      nkb = S // 256
                      P.op('dve', lambda e: e.memset(negmT[0:8, :], 0.0), writes=['negmT'])
                      if nkb > 4:
                          P.op('dve', lambda e, nkb=nkb: e.tensor_reduce(out=kmf[:, 0:nkb], in_=KT[:].rearrange("p (n k) -> p n k", k=256),
                                                                        axis=AX.X, op=ALU.add), reads=['KT'], writes=['kmf'])
                          P.op('dve', lambda e, nkb=nkb: e.tensor_scalar(out=kmb[:, 0:nkb], in0=kmf[:, 0:nkb], scalar1=1.0 / 256, scalar2=None,
                                                                        op0=ALU.mult), reads=['kmf'], writes=['kmb'])
                          P.op('dve', lambda e: e.memset(gm[:], -1e30), writes=['gm'])
                          P.op('dve', lambda e: e.memset(negm[:], 0.0), writes=['negm'])
                          for qt in range(8, NT):
                              ob = qt // 2
                              P.group('pe', [lambda e, qt=qt, nkb=nkb: e.matmul(psb[Qb][:, 0:nkb], QT[:, qt * 128:(qt + 1) * 128], kmb[:, 0:nkb],
                                                                               start=True, stop=True)], reads=['QT', 'kmb'], writes=[('ps', Qb)])
                              P.op('dve', lambda e, ob=ob: e.tensor_copy(out=gm[:, 0:ob], in_=psb[Qb][:, 0:ob]), reads=[('ps', Qb)], writes=['gm'])
                              P.op('dve', lambda e: e.max(out=top8[:], in_=gm[:]), reads=['gm'], writes=['top8'])
                              P.op('dve', lambda e, ob=ob: e.tensor_scalar(out=negm[:, 0:ob], in0=gm[:, 0:ob], scalar1=top8[:, 2:3], scalar2=-NEG,
                                                                          op0=ALU.is_ge, op1=ALU.mult), reads=['gm', 'top8'], writes=['negm'])
                              P.op('dve', lambda e, ob=ob: e.tensor_scalar(out=negm[:, 0:ob], in0=negm[:, 0:ob], scalar1=NEG, scalar2=None,
                                                                          op0=ALU.add), reads=['negm'], writes=['negm'])
                              P.op('dve', lambda e: e.tensor_copy(out=negmb[:], in_=negm[:]), reads=['negm'], writes=['negmb'])
                              P.group('pe', [lambda e: e.transpose(out=tp[0][0:8, 0:128], in_=negmb[:, 0:8], identity=cm(M_ID))],
                                      reads=['negmb', 'cb'], writes=[('tp', 0)])
                              P.op('dve', lambda e, qt=qt: e.tensor_copy(out=negmT[0:8, qt * 128:(qt + 1) * 128], in_=tp[0][0:8, 0:128]),
                                   reads=[('tp', 0)], writes=['negmT'])
                  ot = OT[hd % 2]
                  otk = ('OT', hd % 2)
                  for qb in range(NQB):
                      nkt = 4 * (qb + 1)
                      q0 = qb * 512
                      if g != 's':
                          maps = [(0, 128)] if g != 'd' else [(0, 64), (64, 128)]
                          for mi, (pa, pb) in enumerate(maps):
                              for kt in range(nkt):
                                  c0 = max(0, kt * 128 - q0)
                                  diag = kt * 128 >= q0
                                  sbk = S0 + (kt % 2)
                                  mm = [(psb[sbk][:, c0:512], KT[pa:pb, kt * 128:(kt + 1) * 128], QT[pa:pb, q0 + c0:q0 + 512])]
                                  rd = ['KT', 'QT', 'cb']
                                  if g == 'f':
                                      mm.append((psb[sbk][:, c0:512], fa[0:6, kt * 128:(kt + 1) * 128], fb[0:6, q0 + c0:q0 + 512]))
                                      rd += ['fa', 'fb']
                                  if g == 'm' and qb >= 2:
                                      kb = kt // 2
                                      mm.append((psb[sbk][:, c0:512], cb[0:8, (M_IND + kb) * 128:(M_IND + kb + 1) * 128], negmT[0:8, q0 + c0:q0 + 512]))
                                      rd += ['negmT']
                                  if diag:
                                      mm.append((psb[sbk][:, c0:c0 + 128], cm(M_ID), cm(M_NEGU)))
                                  fns = [lambda e, o=o, a_=a_, b_=b_, i=i, n=len(mm): e.matmul(o, a_, b_, start=(i == 0), stop=(i == n - 1))
                                         for i, (o, a_, b_) in enumerate(mm)]
                                  P.group('pe', fns, reads=rd, writes=[('ps', sbk)])
                                  pi = kt % 3
                                  P.op('act', lambda e, pi=pi, sbk=sbk, c0=c0: e.activation(out=Pt[pi][:, c0:512], in_=psb[sbk][:, c0:512], func=AF.Exp),
                                       reads=[('ps', sbk)], writes=[('P', pi)])
                                  fns = [lambda e, kt=kt, c0=c0, pi=pi, nkt=nkt: e.matmul(psb[Ob][:, c0:512], Vh[:, kt * 128:(kt + 1) * 128], Pt[pi][:, c0:512],
                                                                                         start=(kt == 0), stop=(kt == nkt - 1)),
                                         lambda e, kt=kt, c0=c0, pi=pi, nkt=nkt: e.matmul(psb[Db][:, c0:512], cm(M_ONES), Pt[pi][:, c0:512],
                                                                                         start=(kt == 0), stop=(kt == nkt - 1))]
                                  P.group('pe', fns, reads=[('P', pi), 'Vh', 'cb'], writes=[('ps', Ob), ('ps', Db)])
                              P.op('dve', lambda e: e.reciprocal(out=rden[:], in_=psb[Db][:, :]), reads=[('ps', Db)], writes=['rden'])
                              if g != 'd':
                                  P.op('dve', lambda e, ot=ot, q0=q0: e.tensor_tensor(out=ot[:, q0:q0 + 512], in0=psb[Ob][:, :], in1=rden[:], op=ALU.mult),
                                       reads=[('ps', Ob), 'rden'], writes=[otk])
                              else:
                                  od = o1 if mi == 0 else o2
                                  P.op('dve', lambda e, od=od: e.tensor_tensor(out=od[:], in0=psb[Ob][:, :], in1=rden[:], op=ALU.mult),
                                       reads=[('ps', Ob), 'rden'], writes=['o%d' % mi])
                          if g == 'd':
                              P.op('dve', lambda e: e.scalar_tensor_tensor(out=comb[:], in0=o2[:], scalar=small[:, 8:9], in1=o1[:], op0=ALU.mult, op1=ALU.add),
                                   reads=['o0', 'o1', 'neglam'], writes=['comb'])
                              P.op('act', lambda e: e.activation(out=sq[:], in_=comb[:], func=AF.Square), reads=['comb'], writes=['sq'])
                              P.group('pe', [lambda e: e.matmul(psb[Qb][:, :], onesf[:, 0:128], sq[:], start=True, stop=True)],
                                      reads=['sq', 'onesf'], writes=[('ps', Qb)])
                              P.op('dve', lambda e: e.tensor_scalar(out=rstdb[:], in0=psb[Qb][:, :], scalar1=1.0 / 128, scalar2=1e-5, op0=ALU.mult, op1=ALU.add),
                                   reads=[('ps', Qb)], writes=['rstdb'])
                              P.op('act', lambda e: e.activation(out=rstdb[:], in_=rstdb[:], func=AF.Ln), reads=['rstdb'], writes=['rstdb'])
                              P.op('act', lambda e: e.activation(out=rstdb[:], in_=rstdb[:], func=AF.Exp, scale=-0.5), reads=['rstdb'], writes=['rstdb'])
                              P.op('dve', lambda e, ot=ot, q0=q0: e.scalar_tensor_tensor(out=ot[:, q0:q0 + 512], in0=comb[:], scalar=small[:, 3:4], in1=rstdb[:],
                                                                                         op0=ALU.mult, op1=ALU.mult),
                                   reads=['comb', 'rstdb', 'gsc'], writes=[otk])
                      else:
                          for n, kt in enumerate(range(nkt - 1, -1, -1)):
                              c0 = max(0, kt * 128 - q0)
                              diag = kt * 128 >= q0
                              zb = S0 + (n % 2)
                              ei = n % 2
                              P.group('pe', [lambda e, kt=kt, c0=c0, zb=zb, q0=q0: e.matmul(psb[zb][:, c0:512], KT[:, kt * 128:(kt + 1) * 128],
                                                                                           QT[:, q0 + c0:q0 + 512], start=True, stop=True)],
                                      reads=['KT', 'QT'], writes=[('ps', zb)])
                              P.op('act', lambda e, ei=ei, zb=zb, c0=c0: e.activation(out=ef[ei][:, c0:512], in_=psb[zb][:, c0:512], func=AF.Exp),
                                   reads=[('ps', zb)], writes=[('ef', ei)])
                              P.op('act', lambda e, ei=ei, c0=c0: e.activation(out=spb[ei][:, c0:512], in_=ef[ei][:, c0:512], func=AF.Ln, bias=1.0, scale=1.0),
                                   reads=[('ef', ei)], writes=[('spb', ei)])
                              if diag:
                                  P.op('dve', lambda e, ei=ei, c0=c0: e.tensor_tensor(out=spb[ei][:, c0:c0 + 128], in0=spb[ei][:, c0:c0 + 128],
                                                                                     in1=cm(M_STR), op=ALU.mult),
                                       reads=[('spb', ei), 'cb'], writes=[('spb', ei)])
                              fns = [lambda e, ei=ei, c0=c0, n=n: e.matmul(psb[Xb][:, c0:512], cm(M_NTI), spb[ei][:, c0:512], start=(n == 0), stop=False,
                                                                          skip_group_check=True),
                                     lambda e, kt=kt, c0=c0, q0=q0: e.matmul(psb[Xb][:, c0:512], KT[:, kt * 128:(kt + 1) * 128], QT[:, q0 + c0:q0 + 512],
                                                                             start=False, stop=True, skip_group_check=True)]
                              P.group('pe', fns, reads=[('spb', ei), 'KT', 'QT', 'cb'], writes=[('ps', Xb)])
                              pi = n % 3
                              P.op('act', lambda e, pi=pi, c0=c0: e.activation(out=Pt[pi][:, c0:512], in_=psb[Xb][:, c0:512], func=AF.Exp),
                                   reads=[('ps', Xb)], writes=[('P', pi)])
                              if diag:
                                  P.op('dve', lambda e, pi=pi, c0=c0: e.tensor_tensor(out=Pt[pi][:, c0:c0 + 128], in0=Pt[pi][:, c0:c0 + 128],
                                                                                     in1=cm(M_STR), op=ALU.mult),
                                       reads=[('P', pi), 'cb'], writes=[('P', pi)])
                              fns = []
                              if kt > 0:
                                  fns += [lambda e, kt=kt, c0=c0, q0=q0: e.matmul(psb[Xb][:, c0:512], KTn[:, kt * 128:(kt + 1) * 128], QT[:, q0 + c0:q0 + 512],
                                                                                  start=False, stop=False, skip_group_check=True),
                                          lambda e, ei=ei, c0=c0: e.matmul(psb[Xb][:, c0:512], cm(M_NTE), spb[ei][:, c0:512], start=False, stop=True,
                                                                           skip_group_check=True)]
                              fns.append(lambda e, kt=kt, c0=c0, pi=pi, n=n, nkt=nkt: e.matmul(psb[Ob][:, c0:512], Vh[:, kt * 128:(kt + 1) * 128], Pt[pi][:, c0:512],
                                                                                              start=(n == 0), stop=(n == nkt - 1), skip_group_check=True))
                              P.group('pe', fns, reads=[('P', pi), ('spb', ei), 'KTn', 'QT', 'Vh', 'cb'], writes=[('ps', Xb), ('ps', Ob)])
                          P.op('act', lambda e, ot=ot, q0=q0: e.copy(out=ot[:, q0:q0 + 512], in_=psb[Ob][:, :]), reads=[('ps', Ob)], writes=[otk])
                  P.op('pool', lambda e, ot=ot, hd=hd: e.dma_start(out=mixT[hd, :, :], in_=ot[:]), reads=[otk], writes=[('mix', hd)])
          P.barrier()
          if stop == 'p3':
              break

          for half in range(NHALF):
              tok0 = half * TBA
              for hd in range(NH):
                  P.op('pool', lambda e, hd=hd, tok0=tok0: e.dma_start(out=hT[:, hd, :], in_=mixT[hd, :, tok0:tok0 + TBA]),
                       reads=[('mix', hd)], writes=[('hT', tl) for tl in range(NTA)])
              for c in range(D // 512):
                  w, wk = wload([(lambda w: w[:, :, :], wsrc(w_out, l, c * 512, 512))])
                  for tl in range(NTA):
                      r0 = tok0 + tl * 128
                      pt, pk = nextps()
                      xi = (c * NTA + tl) % 4
                      P.op('pool', lambda e, xi=xi, r0=r0, c=c, xsrc=xsrc: e.dma_start(out=xa[xi][:], in_=xsrc[r0:r0 + 128, c * 512:(c + 1) * 512]),
                           reads=[(xkey, r0 // 128, c)], writes=[('xa', xi)])
                      fns = [lambda e, kc=kc, tl=tl, pt=pt, w=w: e.matmul(pt[:, :], hT[:, kc, tl * 128:(tl + 1) * 128], w[:, kc, :],
                                                                          start=(kc == 0), stop=(kc == KC - 1)) for kc in range(KC)]
                      P.group('pe', fns, reads=[wk, ('hT', tl)], writes=[pk])
                      P.op('dve', lambda e, xi=xi, pt=pt: e.tensor_tensor(out=xa[xi][:], in0=pt[:, :], in1=xa[xi][:], op=ALU.add),
                           reads=[pk, ('xa', xi)], writes=[('xa', xi)])
                      P.op('pool', lambda e, xi=xi, r0=r0, c=c: e.dma_start(out=xres[r0:r0 + 128, c * 512:(c + 1) * 512], in_=xa[xi][:]),
                           reads=[('xa', xi)], writes=[('xres', r0 // 128, c)])
          P.barrier()
          if stop == 'p4':
              break

          gk = load_gain(2 * l + 1)
          for tb in range(S // TB5):
              tok0 = tb * TB5
              norm_T(xres, tok0, NT5, gk, 'xres')
              hkeys = [('hT', tl) for tl in range(NT5)]
              for (fa0, fa1) in passes:
                  for f0 in range(fa0, fa1, FG):
                      wg, wgk = wload([(lambda w: w[:, :, 0:FG * 128], wsrc(w_gate, l, f0 * 128, FG * 128))])
                      wu, wuk = wload([(lambda w: w[:, :, 0:FG * 128], wsrc(w_up, l, f0 * 128, FG * 128))])
                      for fc in range(FG):
                          fl = f0 + fc - fa0
                          for sub in range(NSUB):
                              pg, pgk = nextps(2)
                              fns = [lambda e, kc=kc, fc=fc, pg=pg, wg=wg, sub=sub: e.matmul(
                                  pg[:, :], wg[:, kc, fc * 128:(fc + 1) * 128], hT[:, kc, sub * 512:(sub + 1) * 512],
                                  start=(kc == 0), stop=(kc == KC - 1)) for kc in range(KC)]
                              P.group('pe', fns, reads=[wgk] + hkeys[sub * 4:(sub + 1) * 4], writes=[pgk])
                              pu, puk = nextps(2)
                              fns = [lambda e, kc=kc, fc=fc, pu=pu, wu=wu, sub=sub: e.matmul(
                                  pu[:, :], wu[:, kc, fc * 128:(fc + 1) * 128], hT[:, kc, sub * 512:(sub + 1) * 512],
                                  start=(kc == 0), stop=(kc == KC - 1)) for kc in range(KC)]
                              P.group('pe', fns, reads=[wuk] + hkeys[sub * 4:(sub + 1) * 4], writes=[puk])
                              si = (fl * NSUB + sub) % 2
                              P.op('act', lambda e, si=si, pg=pg: e.activation(out=sg[si][:], in_=pg[:, :], func=AF.Silu),
                                   reads=[pgk], writes=[('sg', si)])
                              P.op('dve', lambda e, si=si, pu=pu, fl=fl, sub=sub: e.tensor_tensor(
                                  out=actT_ap(fl, sub * 512, (sub + 1) * 512), in0=sg[si][:], in1=pu[:, :], op=ALU.mult),
                                  reads=[('sg', si), puk], writes=[('actT', fl, sub)])
                  items = [(c_, tg_, f0_) for c_ in range(D // 512) for tg_ in range(NT5 // 4) for f0_ in range(fa0, fa1, FG)]

                  def wd_load(it):
                      c_, tg_, f0_ = it
                      return wload([(lambda w: w[:, 0:FG, :],
                                     w_down[l, f0_ * 128:(f0_ + FG) * 128, c_ * 512:(c_ + 1) * 512].rearrange("(f p) c -> p f c", p=128))])
                  pre = {0: wd_load(items[0])}
                  for c in range(D // 512):
                      for tg in range(NT5 // 4):
                          for f0 in range(fa0, fa1, FG):
                              ii = items.index((c, tg, f0))
                              wd, wdk = pre.pop(ii)
                              if ii + 1 < len(items):
                                  pre[ii + 1] = wd_load(items[ii + 1])
                              fns = []
                              for f in range(FG):
                                  fl = f0 + f - fa0
                                  for t in range(4):
                                      tt = tg * 4 + t
                                      fns.append(lambda e, f=f, fl=fl, t=t, tt=tt, wd=wd, st=(f0 == fa0 and f == 0), sp_=(f0 + f == fa1 - 1): e.matmul(
                                          psb[2 + t][:, :], actT_ap(fl, tt * 128, (tt + 1) * 128), wd[:, f, :], start=st, stop=sp_, skip_group_check=True))
                              P.group('pe', fns, reads=[wdk] + [('actT', f0 + f - fa0, s_) for f in range(FG) for s_ in range(NSUB)],
                                      writes=[('ps', 2 + t) for t in range(4)])
                          for t in range(4):
                              r0 = tok0 + (tg * 4 + t) * 128
                              xi = t
                              P.op('pool', lambda e, xi=xi, r0=r0, c=c: e.dma_start(out=xa2[xi][:], in_=xres[r0:r0 + 128, c * 512:(c + 1) * 512]),
                                   reads=[('xres', r0 // 128, c)], writes=[('xa2', xi)])
                              P.op('dve', lambda e, xi=xi, t=t: e.tensor_tensor(out=xa2[xi][:], in0=psb[2 + t][:, :], in1=xa2[xi][:], op=ALU.add),
                                   reads=[('ps', 2 + t), ('xa2', xi)], writes=[('xa2', xi)])
                              P.op('pool', lambda e, xi=xi, r0=r0, c=c: e.dma_start(out=xres[r0:r0 + 128, c * 512:(c + 1) * 512], in_=xa2[xi][:]),
                                   reads=[('xa2', xi)], writes=[('xres', r0 // 128, c)])
          P.barrier()
      except _Stop:
        P.barrier()
        break

    gk = load_gain(2 * L)
    for t in range(NT if stop is None else 0):
        r0 = t * 128
        b = t % 2
        P.op('pool', lambda e, b=b, r0=r0: e.dma_start(out=xt[b][:], in_=xres[r0:r0 + 128, :]),
             reads=[('xres', t, c) for c in range(D // 512)], writes=[('xt', b)])
        P.op('act', lambda e, b=b, t=t: e.activation(out=hb[b][:], in_=xt[b][:], func=AF.Square, accum_out=ssq[:, t:t + 1]),
             reads=[('xt', b)], writes=[('hb', b), ('ssq', t)])
        P.op('dve', lambda e, t=t: e.tensor_scalar(out=ssq[:, t:t + 1], in0=ssq[:, t:t + 1], scalar1=1.0 / D, scalar2=1e-6, op0=ALU.mult, op1=ALU.add),
             reads=[('ssq', t)], writes=[('ssq', t)])
        P.op('act', lambda e, t=t: e.activation(out=ssq[:, t:t + 1], in_=ssq[:, t:t + 1], func=AF.Ln),
             reads=[('ssq', t)], writes=[('ssq', t)])
        P.op('act', lambda e, t=t: e.activation(out=ssq[:, t:t + 1], in_=ssq[:, t:t + 1], func=AF.Exp, scale=-0.5),
             reads=[('ssq', t)], writes=[('ssq', t)])
        P.op('dve', lambda e, b=b, t=t: e.scalar_tensor_tensor(out=xt[b][:], in0=xt[b][:], scalar=ssq[:, t:t + 1], in1=gbuf[:], op0=ALU.mult, op1=ALU.mult),
             reads=[('xt', b), ('ssq', t), gk], writes=[('xt', b)])
        P.op('pool', lambda e, b=b, r0=r0: e.dma_start(out=y[r0:r0 + 128, :], in_=xt[b][:]), reads=[('xt', b)], writes=[('y', t)])
    P.barrier()

    with nc.Block() as block:
        @block.tensor
        def _(e):
            P.replay('pe', e)

        @block.scalar
        def _(e):
            P.replay('act', e)

        @block.vector
        def _(e):
            P.replay('dve', e)

        @block.gpsimd
        def _(e):
            P.replay('sp', e)

        @block.sync
        def _(e):
            P.replay('pool', e)
    es.close()
    return nc


def host_inputs(cfg, x, w_in, b_fgate, w_out, diff_lq1, diff_lk1, diff_lq2, diff_lk2, diff_subln,
                attn_norm, w_gate, w_up, w_down, ffn_norm, final_norm):
    L, HG, D = cfg.L, cfg.HG, cfg.D
    f32 = np.float32
    w_in = np.ascontiguousarray(np.asarray(w_in, f32))
    fgc = cfg.off['fg']
    w_fg = np.zeros((L, D, 128), f32)
    b_rep = np.zeros((L, 128, 1), f32)
    for h in range(HG):
        w_fg[:, :, h * 32:(h + 1) * 32] = w_in[:, :, fgc + h:fgc + h + 1]
        b_rep[:, h * 32:(h + 1) * 32, 0] = np.asarray(b_fgate, f32)[:, h][:, None]
    lam = np.stack([np.asarray(a, f32) for a in (diff_lq1, diff_lk1, diff_lq2, diff_lk2)], 1)
    lam_in = np.ascontiguousarray(np.broadcast_to(lam.reshape(1, L * 256), (128, L * 256)))
    gsub = np.ascontiguousarray(np.asarray(diff_subln, f32)[:, :, None])
    gl = []
    for l in range(L):
        gl += [np.asarray(attn_norm, f32)[l], np.asarray(ffn_norm, f32)[l]]
    gl.append(np.asarray(final_norm, f32))
    gains = np.ascontiguousarray(np.broadcast_to(np.stack(gl, 0)[:, None, :], (2 * L + 1, 128, D)))
    rope, cbm, cfm, onesf = host_consts(cfg)
    common = dict(w_in=w_in, w_fg=w_fg, b_rep=b_rep, w_out=np.ascontiguousarray(np.asarray(w_out, f32)), lam_in=lam_in, gsub=gsub,
                  gains=gains, w_gate=np.ascontiguousarray(np.asarray(w_gate, f32)), w_up=np.ascontiguousarray(np.asarray(w_up, f32)),
                  w_down=np.ascontiguousarray(np.asarray(w_down, f32)), rope=rope, cb=cbm, cf=cfm, onesf=onesf)
    x = np.asarray(x, f32)
    return [dict(common, x=np.ascontiguousarray(x[b])) for b in range(x.shape[0])]


def kernel(**inputs):
    cfg = Cfg()
    in_maps = host_inputs(cfg, **inputs)
    nc = build(cfg)
    res = run_bass_kernel_spmd(nc, in_maps, core_ids=list(range(len(in_maps))))
    return np.stack([np.asarray(r["y"], np.float32) for r in res.results], 0)
```
